# Optimizing a Trainium2 kernel written in Bass

```python
import jax, jax.numpy as jnp
from jax import lax
import numpy as np

D_MODEL = 1024
BATCH = 8
SEQ = 2048
DEPTH = 2
DEC_BATCH = 8
DEC_SEQ = 64
PAST_LEN = 2048

CHUNK = 64
N_MIXERS = 2
N_CONV_LAYERS = (DEPTH + N_MIXERS - 1) // N_MIXERS
N_ATTN_LAYERS = DEPTH // N_MIXERS
CONV_WIDTH = 3
N_HEADS = 16
HEAD_DIM = D_MODEL // N_HEADS
D_FF = 2816
Q_BLOCK = 128
FFN_RESIDUAL = 0.5
NORM_EPS = 1e-6
FORGET_BIAS_MEAN = 3.0
NEG_INF = -1e30

kernel_name = "conv_fox_macaron_stream_step"


def rms_norm(x, g):
    xf = x.astype(jnp.float32)
    y = xf * lax.rsqrt(jnp.mean(xf * xf, axis=-1, keepdims=True) + NORM_EPS)
    return (y * g.astype(jnp.float32)).astype(x.dtype)


def swiglu_ffn(x, g, w_in, w_out):
    a, b = jnp.split(rms_norm(x, g) @ w_in, 2, axis=-1)
    return (jax.nn.silu(a) * b) @ w_out


def conv_mixer(xn, hist, w_in, w_k, w_out):
    gate_b, gate_c, h = jnp.split(xn @ w_in, 3, axis=-1)
    u = gate_c * h
    u_full = jnp.concatenate([hist.astype(u.dtype), u], axis=1)
    t_len = u.shape[1]
    conv = w_k[0] * u_full[:, 0:t_len]
    for j in range(1, CONV_WIDTH):
        conv = conv + w_k[j] * u_full[:, j:j + t_len]
    y = (gate_b * conv) @ w_out
    return y, u_full[:, t_len:]


def fox_project(xn, w_in, b_f, gq, gk):
    b, t, _ = xn.shape
    proj = xn @ w_in
    q = proj[..., :D_MODEL].reshape(b, t, N_HEADS, HEAD_DIM)
    k = proj[..., D_MODEL:2 * D_MODEL].reshape(b, t, N_HEADS, HEAD_DIM)
    v = proj[..., 2 * D_MODEL:3 * D_MODEL].reshape(b, t, N_HEADS, HEAD_DIM)
    f_logit = proj[..., 3 * D_MODEL:].astype(jnp.float32) + b_f.astype(jnp.float32)
    q = rms_norm(q, gq)
    k = rms_norm(k, gk)
    logf = jax.nn.log_sigmoid(f_logit)
    return q, k, v, logf


def fox_attend(q, k, v, c_q, c_k, q_pos, k_pos):
    s = jnp.einsum('bqhd,bkhd->bhqk', q, k, preferred_element_type=jnp.float32) * (HEAD_DIM ** -0.5)
    decay = jnp.transpose(c_q, (0, 2, 1))[..., :, None] - jnp.transpose(c_k, (0, 2, 1))[..., None, :]
    mask = k_pos[None, :] <= q_pos[:, None]
    s = jnp.where(mask, s + decay, NEG_INF)
    p = jax.nn.softmax(s, axis=-1)
    return jnp.einsum('bhqk,bkhd->bqhd', p.astype(v.dtype), v)


def fox_prompt(q, k, v, logf):
    b, s_len, h, d = q.shape
    nb = s_len // Q_BLOCK
    c = jnp.cumsum(logf, axis=1)
    qb = jnp.transpose(q.reshape(b, nb, Q_BLOCK, h, d), (1, 0, 2, 3, 4))
    cb = jnp.transpose(c.reshape(b, nb, Q_BLOCK, h), (1, 0, 2, 3))
    k_pos = jnp.arange(s_len)

    def block(args):
        idx, q_blk, c_blk = args
        q_pos = idx * Q_BLOCK + jnp.arange(Q_BLOCK)
        return fox_attend(q_blk, k, v, c_blk, c, q_pos, k_pos)

    out = lax.map(block, (jnp.arange(nb), qb, cb))
    return jnp.transpose(out, (1, 0, 2, 3, 4)).reshape(b, s_len, h * d)


def fox_sample(q, k, v, logf, past_k, past_v, past_logf):
    b, t_len, h, d = q.shape
    past = past_k.shape[1]
    k_all = jnp.concatenate([past_k.astype(k.dtype), k], axis=1)
    v_all = jnp.concatenate([past_v.astype(v.dtype), v], axis=1)
    c_all = jnp.cumsum(jnp.concatenate([past_logf.astype(jnp.float32), logf], axis=1), axis=1)
    q_pos = past + jnp.arange(t_len)
    k_pos = jnp.arange(past + t_len)
    out = fox_attend(q, k_all, v_all, c_all[:, past:], c_all, q_pos, k_pos)
    return out.reshape(b, t_len, h * d)


def setup_inputs(seed: int = 0) -> dict:
    key = jax.random.key(seed)
    ks = jax.random.split(key, 18)
    nrm = lambda k, shape: jax.random.normal(k, shape, jnp.float32)
    D, H, F = D_MODEL, N_HEADS, D_FF
    return {
        "x_prompt": nrm(ks[0], (BATCH, SEQ, D)),
        "x_sample": nrm(ks[1], (DEC_BATCH, DEC_SEQ, D)),
        "state_conv": nrm(ks[2], (N_CONV_LAYERS, DEC_BATCH, CONV_WIDTH - 1, D)),
        "cache_k": nrm(ks[3], (N_ATTN_LAYERS, DEC_BATCH, PAST_LEN, H, HEAD_DIM)),
        "cache_v": nrm(ks[4], (N_ATTN_LAYERS, DEC_BATCH, PAST_LEN, H, HEAD_DIM)),
        "cache_logf": jax.nn.log_sigmoid(FORGET_BIAS_MEAN + 0.5 * nrm(ks[5], (N_ATTN_LAYERS, DEC_BATCH, PAST_LEN, H))),
        "norm_ffn": 1.0 + 0.05 * nrm(ks[6], (DEPTH, 2, D)),
        "ffn_w_in": nrm(ks[7], (DEPTH, 2, D, 2 * F)) * D ** -0.5,
        "ffn_w_out": nrm(ks[8], (DEPTH, 2, F, D)) * F ** -0.5,
        "norm_mix": 1.0 + 0.05 * nrm(ks[9], (DEPTH, D)),
        "conv_w_in": nrm(ks[10], (N_CONV_LAYERS, D, 3 * D)) * D ** -0.5,
        "conv_w": nrm(ks[11], (N_CONV_LAYERS, CONV_WIDTH, D)) * CONV_WIDTH ** -0.5,
        "conv_w_out": nrm(ks[12], (N_CONV_LAYERS, D, D)) * D ** -0.5,
        "attn_w_in": nrm(ks[13], (N_ATTN_LAYERS, D, 3 * D + H)) * D ** -0.5,
        "attn_b_f": FORGET_BIAS_MEAN + 0.5 * nrm(ks[14], (N_ATTN_LAYERS, H)),
        "q_norm": 1.0 + 0.05 * nrm(ks[15], (N_ATTN_LAYERS, HEAD_DIM)),
        "k_norm": 1.0 + 0.05 * nrm(ks[16], (N_ATTN_LAYERS, HEAD_DIM)),
        "attn_w_out": nrm(ks[17], (N_ATTN_LAYERS, D, D)) * D ** -0.5,
    }


def reference(x_prompt, x_sample, state_conv, cache_k, cache_v, cache_logf,
              norm_ffn, ffn_w_in, ffn_w_out, norm_mix,
              conv_w_in, conv_w, conv_w_out,
              attn_w_in, attn_b_f, q_norm, k_norm, attn_w_out):
    yp, ys = x_prompt, x_sample
    conv_p, conv_s = [], []
    kp_l, vp_l, fp_l, ks_l, vs_l, fs_l = [], [], [], [], [], []
    for i in range(DEPTH):
        li = i // N_MIXERS
        yp = yp + FFN_RESIDUAL * swiglu_ffn(yp, norm_ffn[i, 0], ffn_w_in[i, 0], ffn_w_out[i, 0])
        ys = ys + FFN_RESIDUAL * swiglu_ffn(ys, norm_ffn[i, 0], ffn_w_in[i, 0], ffn_w_out[i, 0])
        xn_p = rms_norm(yp, norm_mix[i])
        xn_s = rms_norm(ys, norm_mix[i])
        if i % N_MIXERS == 0:
            hist_p = jnp.zeros((yp.shape[0], CONV_WIDTH - 1, D_MODEL), yp.dtype)
            mp, st_p = conv_mixer(xn_p, hist_p, conv_w_in[li], conv_w[li], conv_w_out[li])
            ms, st_s = conv_mixer(xn_s, state_conv[li], conv_w_in[li], conv_w[li], conv_w_out[li])
            conv_p.append(st_p)
            conv_s.append(st_s)
        else:
            qp, kp, vp, fp = fox_project(xn_p, attn_w_in[li], attn_b_f[li], q_norm[li], k_norm[li])
            qs, kq, vq, fq = fox_project(xn_s, attn_w_in[li], attn_b_f[li], q_norm[li], k_norm[li])
            mp = fox_prompt(qp, kp, vp, fp) @ attn_w_out[li]
            ms = fox_sample(qs, kq, vq, fq, cache_k[li], cache_v[li], cache_logf[li]) @ attn_w_out[li]
            kp_l.append(kp); vp_l.append(vp); fp_l.append(fp)
            ks_l.append(kq); vs_l.append(vq); fs_l.append(fq)
        yp = yp + mp
        ys = ys + ms
        yp = yp + FFN_RESIDUAL * swiglu_ffn(yp, norm_ffn[i, 1], ffn_w_in[i, 1], ffn_w_out[i, 1])
        ys = ys + FFN_RESIDUAL * swiglu_ffn(ys, norm_ffn[i, 1], ffn_w_in[i, 1], ffn_w_out[i, 1])
    return (yp, ys,
            jnp.stack(conv_p), jnp.stack(conv_s),
            jnp.stack(kp_l), jnp.stack(vp_l), jnp.stack(fp_l),
            jnp.stack(ks_l), jnp.stack(vs_l), jnp.stack(fs_l))
```

```python
from contextlib import ExitStack
import numpy as np
import concourse.bass as bass
import concourse.mybir as mybir
from concourse.bass_utils import run_bass_kernel_spmd

F32 = mybir.dt.float32
BF16 = mybir.dt.bfloat16
AF = mybir.ActivationFunctionType
ALU = mybir.AluOpType
AX = mybir.AxisListType

D = 1024
KC = 8
FF = 2816
H = 16
HD = 64
PL = 2048
SL = 64
NT = 1088
EPS = 1e-6
NCORES = 8


class Trk:
    def __init__(self, nc, es):
        self.nc = nc
        self.engs = {"pe": nc.tensor, "act": nc.scalar, "dve": nc.vector, "pool": nc.gpsimd, "sp": nc.sync}
        self.sem = {k: es.enter_context(nc.semaphore("s_" + k)) for k in self.engs}
        self.cnt = {k: 0 for k in self.engs}
        self.known = {k: {} for k in self.engs}
        self.res = {}
        self.dsems = {q: [es.enter_context(nc.semaphore("d_%s%d" % (q, i))) for i in range(n)]
                      for q, n in (("sp", 14), ("pool", 14))}
        self.dval = {q: [0] * len(v) for q, v in self.dsems.items()}
        self.dnext = {q: 0 for q in self.dsems}
        self.outs = []

    def _waits(self, eng, reads, writes):
        waits = {}

        def need(ev, same_ok):
            if ev is None:
                return
            sem, val, e = ev
            if e == eng and same_ok:
                return
            k = id(sem)
            if self.known[eng].get(k, 0) >= val:
                return
            if k not in waits or waits[k][1] < val:
                waits[k] = (sem, val)

        for r in reads:
            st = self.res.get(r)
            if st:
                need(st[0], False)
                if isinstance(r, tuple) and r[0] == "ps":
                    for ev in st[1].values():
                        need(ev, True)
        for w in writes:
            st = self.res.get(w)
            if st:
                need(st[0], True)
                for ev in st[1].values():
                    need(ev, True)
        for k, (sem, val) in waits.items():
            self.engs[eng].wait_ge(sem, val)
            self.known[eng][k] = val

    def _record(self, ev, who, reads, writes):
        for r in reads:
            st = self.res.setdefault(r, [None, {}])
            st[1][who] = ev
        for w in writes:
            self.res[w] = [ev, {}]

    def op(self, eng, fn, reads=(), writes=()):
        self._waits(eng, reads, writes)
        ins = fn(self.engs[eng])
        self.cnt[eng] += 1
        ins.then_inc(self.sem[eng], 1)
        ev = (self.sem[eng], self.cnt[eng], eng)
        self._record(ev, eng, reads, writes)
        return ev

    def dma(self, q, out, in_, reads=(), writes=(), is_out=False, **kw):
        i = self.dnext[q]
        self.dnext[q] = (i + 1) % len(self.dsems[q])
        sem = self.dsems[q][i]
        prev = self.dval[q][i]
        if prev > 0 and self.known[q].get(id(sem), 0) < prev:
            self.engs[q].wait_ge(sem, prev)
            self.known[q][id(sem)] = prev
        self._waits(q, reads, writes)
        ins = self.engs[q].dma_start(out=out, in_=in_, **kw)
        ins.then_inc(sem, 16)
        self.dval[q][i] = prev + 16
        ev = (sem, prev + 16, "dma_%s_%d_%d" % (q, i, prev))
        self._record(ev, ev[2], reads, writes)
        if is_out:
            self.outs.append(ev)
        return ev

    def reset_for_write(self, eng, keys):
        self._waits(eng, [], keys)
        for k in keys:
            self.res[k] = [None, {}]

    def finish(self):
        for sem, val, _ in self.outs:
            if self.known["sp"].get(id(sem), 0) < val:
                self.engs["sp"].wait_ge(sem, val)
                self.known["sp"][id(sem)] = val


class Ring:
    def __init__(self, nslots, plan):
        self.n = nslots
        self.plan = plan
        self.nl = 0
        self.nu = 0

    def prefetch(self):
        while self.nl < len(self.plan) and self.nl < self.nu + self.n:
            self.plan[self.nl](self.nl % self.n)
            self.nl += 1

    def acquire(self, off=0):
        self.prefetch()
        assert self.nl > self.nu + off
        return (self.nu + off) % self.n

    def release(self):
        self.nu += 1
        self.prefetch()


def build_nc():
    nc = bass.Bass("TRN2", target_bir_lowering=False)
    dt = lambda n, s, k="ExternalInput": nc.dram_tensor(n, s, F32, kind=k).ap()
    xp_d = dt("xp", [PL, D]); xs_d = dt("xs", [SL, D]); sconv_d = dt("sconv", [2, D])
    ck_d = dt("ck", [PL, D]); cv_d = dt("cv", [PL, D]); clf_d = dt("clf", [PL, H])
    nffn_d = dt("norm_ffn", [2, 2, D]); wfi_d = dt("ffn_w_in", [2, 2, D, 2 * FF]); wfo_d = dt("ffn_w_out", [2, 2, FF, D])
    nmix_d = dt("norm_mix", [2, D]); cwi_d = dt("conv_w_in", [D, 3 * D]); cw_d = dt("conv_w", [3, D])
    cwo_d = dt("conv_w_out", [D, D]); awi_d = dt("attn_w_in", [D, 3 * D + H]); abf_d = dt("attn_b_f", [H])
    qn_d = dt("q_norm", [HD]); kn_d = dt("k_norm", [HD]); awo_d = dt("attn_w_out", [D, D])
    O = "ExternalOutput"
    yp_d = dt("yp", [PL, D], O); ys_d = dt("ys", [SL, D], O); csp_d = dt("csp", [2, D], O); css_d = dt("css", [2, D], O)
    kp_d = dt("kp", [PL, D], O); vp_d = dt("vp", [PL, D], O); lfp_d = dt("lfp", [PL, H], O)
    ks_d = dt("ks", [SL, D], O); vs_d = dt("vs", [SL, D], O); lfs_d = dt("lfs", [SL, H], O)

    with ExitStack() as es:
        sb = lambda n, s, d=F32: es.enter_context(nc.sbuf_tensor(n, s, d))
        x = sb("x", [128, 9, D])
        xnT = sb("xnT", [128, KC, NT], BF16)
        mT = sb("mT", [128, KC, NT], BF16)
        gT = sb("gT", [128, 12 * NT], BF16)
        Wo = sb("Wo", [128, 12, D], BF16)
        ringb = [sb("ring%d" % i, [128, 4096], BF16) for i in range(3)]
        gtile = [sb("gtile%d" % i, [128, D]) for i in range(2)]
        xns = [sb("xns%d" % i, [128, D], BF16) for i in range(3)]
        sq = sb("sq", [128, 512])
        stmp = [sb("stmp%d" % i, [128, 512]) for i in range(3)]
        ub = [sb("ub%d" % i, [128, 516]) for i in range(2)]
        tA = sb("tA", [128, 512])
        kTr = [sb("kTr%d" % i, [128, 17 * 128], BF16) for i in range(2)]
        qTr = [sb("qTr%d" % i, [128, NT], BF16) for i in range(2)]
        PT = [sb("PT%d" % i, [128, 512], BF16) for i in range(3)]
        Otok = sb("Otok", [128, 9, 256], BF16)
        kst = [sb("kst%d" % i, [128, 256]) for i in range(2)]
        vst = [sb("vst%d" % i, [128, 256]) for i in range(2)]
        small = [sb("small%d" % i, [128, 64]) for i in range(6)]
        lfst = [sb("lfst%d" % i, [128, 16]) for i in range(5)]
        c_p = sb("c_p", [128, 16, 16]); nb_p = sb("nb_p", [128, 16, 16])
        c_s = sb("c_s", [128, 17, 16]); nb_s = sb("nb_s", [128, 17, 16])
        cs_p = sb("cs_p", [128, 16, 16, 3], BF16); cs_s = sb("cs_s", [128, 1, 16, 3], BF16)
        lfc = sb("lfc", [128, 16, 16])
        R_p = sb("R_p", [128, 16]); R_s = sb("R_s", [128, 16])
        ident = sb("ident", [128, 128], BF16); tri = sb("tri", [128, 128], BF16)
        identf = sb("identf", [128, 128]); U = sb("U", [128, 128]); ones = sb("ones", [128, 128])
        epsT = sb("epsT", [128, 1]); bfT = sb("bfT", [128, 16]); gq8 = sb("gq8", [128, 64]); gkT = sb("gkT", [128, 64])
        negM16 = sb("negM16", [128, 16]); mtmp = sb("mtmp", [128, 8])
        histp = sb("histp", [128, 2, 8]); hsin = sb("hsin", [128, 2, 8]); hsout = sb("hsout", [128, 2, 8])
        wk = sb("wk", [128, 3, 8]); oneT = sb("oneT", [128, 1])
        ps = [es.enter_context(nc.psum_tensor("ps%d" % i, [128, 512], F32)) for i in range(8)]
        T = Trk(nc, es)

        KG0 = 0; VG0 = 17 * 256; QG0 = VG0 + 17 * 4 * 66
        Kg = lambda: gT[:, KG0:KG0 + 17 * 256].rearrange("p (t c) -> p t c", c=256)
        Vg = lambda: gT[:, VG0:VG0 + 17 * 264].rearrange("p (t h c) -> p t h c", h=4, c=66)
        Qg = lambda: gT[:, QG0:QG0 + 9 * 272].rearrange("p (t h c) -> p t h c", h=4, c=68)
        gTv = lambda: gT[:, :].rearrange("p (f n) -> p f n", n=NT)

        st = {"ta": 0, "pb": 0, "ev": 0, "ss": 0, "stmp": 0, "xns": 0, "ub": 0, "pt": 0, "ob": 0, "kst": 0, "lf": 0, "ktr": 0, "qtr": 0}

        def rr(name, n):
            v = st[name]
            st[name] = (v + 1) % n
            return v

        st["npb"] = 8

        def pbank():
            v = st["pb"] % st["npb"]
            st["pb"] = (v + 1) % st["npb"]
            return v
        eveng = lambda: ("act", "dve")[rr("ev", 2)]

        def copy_op(eng, out, in_, reads, writes):
            if eng == "act":
                return T.op("act", lambda e: e.copy(out=out, in_=in_), reads, writes)
            return T.op(eng, lambda e: e.tensor_copy(out=out, in_=in_), reads, writes)

        def mm_group(out_ap, pairs, reads, writes):
            def fn(pe):
                n = len(pairs)
                for i, (l, r) in enumerate(pairs):
                    ins = pe.matmul(out_ap, l, r, start=(i == 0), stop=(i == n - 1))
                return ins
            return T.op("pe", fn, reads, writes)

        prologue_q = []
        T.op("pool", lambda e: e.memset(identf[:], 0.0), [], ["identf"])
        T.op("pool", lambda e: e.affine_select(out=identf[:], in_=identf[:], pattern=[[-1, 128]], compare_op=ALU.not_equal,
                                               fill=1.0, base=0, channel_multiplier=1), ["identf"], ["identf"])
        T.op("pool", lambda e: e.memset(epsT[:], EPS), [], ["eps"])
        T.op("pool", lambda e: e.tensor_copy(out=ident[:], in_=identf[:]), ["identf"], ["ident"])

        def consts_late():
            T.op("pool", lambda e: e.memset(U[:], 1.0), [], ["U"])
            T.op("pool", lambda e: e.affine_select(out=U[:], in_=U[:], pattern=[[1, 128]], compare_op=ALU.is_ge,
                                                   fill=0.0, base=0, channel_multiplier=-1), ["U"], ["U"])
            T.op("pool", lambda e: e.memset(ones[:], 1.0), [], ["ones"])
            T.op("pool", lambda e: e.memset(oneT[:], 1.0), [], ["oneT"])
            T.op("pool", lambda e: e.tensor_copy(out=tri[:], in_=U[:]), ["U"], ["tri"])
            T.op("pool", lambda e: e.memset(R_p[:], 0.0), [], ["R_p"])
            T.op("pool", lambda e: e.memset(R_s[:], 0.0), [], ["R_s"])
            T.op("pool", lambda e: e.memset(histp[:], 0.0), [], ["histp"])
            for i in range(2):
                T.op("pool", lambda e, i=i: e.memset(kTr[i][:], 1.0), [], [("kTr", i, j) for j in range(5)])
            T.dma("sp", bfT[:], abf_d.partition_broadcast(128), [], ["bfT"])
            T.dma("sp", gq8[:], qn_d.partition_broadcast(128), [], ["gq8"])
            T.dma("sp", gkT[:], kn_d.partition_broadcast(128), [], ["gkT"])
            for j in range(3):
                T.dma("sp", wk[:, j, :], cw_d[j].rearrange("(dc p) -> p dc", p=128), [], [("wk", j)], allow_slow_non_contiguous=True)
            for t in range(2):
                T.dma("sp", hsin[:, t, :], sconv_d[t].rearrange("(dc p) -> p dc", p=128), [], [("hsin", t)], allow_slow_non_contiguous=True)
            T.dma("sp", lfc[:], clf_d.rearrange("(t p) h -> p t h", p=128), [], ["lfc"])
            def m_compute():
                sa_ = rr("ss", 6); sb_ = rr("ss", 6)
                ka_ = ("sm", sa_); kb_ = ("sm", sb_)
                T.op("dve", lambda e: e.tensor_tensor(out=small[sa_][:, 0:64], in0=gq8[:], in1=gq8[:], op=ALU.mult), ["gq8"], [ka_])
                T.op("dve", lambda e: e.reduce_max(out=mtmp[:, 0:1], in_=small[sa_][:, 0:64], axis=AX.X), [ka_], ["mtmp"])
                T.op("dve", lambda e: e.tensor_tensor(out=small[sb_][:, 0:64], in0=gkT[:], in1=gkT[:], op=ALU.mult), ["gkT"], [kb_])
                T.op("dve", lambda e: e.reduce_max(out=mtmp[:, 1:2], in_=small[sb_][:, 0:64], axis=AX.X), [kb_], ["mtmp"])
                T.op("dve", lambda e: e.tensor_tensor(out=mtmp[:, 2:3], in0=mtmp[:, 0:1], in1=mtmp[:, 1:2], op=ALU.mult), ["mtmp"], ["mtmp"])
                T.op("act", lambda e: e.activation(out=mtmp[:, 3:4], in_=mtmp[:, 2:3], func=AF.Ln, scale=64.0), ["mtmp"], ["mtmp2"])
                T.op("act", lambda e: e.activation(out=mtmp[:, 4:5], in_=mtmp[:, 3:4], func=AF.Exp, scale=0.5), ["mtmp2"], ["mtmp3"])
                T.op("dve", lambda e: e.tensor_scalar(out=negM16[:], in0=ones[:, 0:16], scalar1=mtmp[:, 4:5], scalar2=None,
                                                      op0=ALU.mult), ["mtmp3", "ones"], ["negM16"])
                T.op("dve", lambda e: e.tensor_scalar(out=negM16[:], in0=negM16[:], scalar1=-1.0, scalar2=None,
                                                      op0=ALU.mult), ["negM16"], ["negM16"])
                T.op("dve", lambda e: e.tensor_scalar(out=gq8[:], in0=gq8[:], scalar1=0.125, scalar2=None, op0=ALU.mult), ["gq8", ka_], ["gq8"])


            prologue_q.append(m_compute)

        def wview(w2d, c0, n):
            return w2d[:, c0:c0 + n].rearrange("(kc p) n -> p kc n", p=128)

        ring_plan = []
        wo_plan = []
        RK = lambda slot: [("ring", slot, 0), ("ring", slot, 1), ("ring", slot, 2)]
        WK = [("wo", 0), ("wo", 1)]

        def ld_ffn_in(i, j, t):
            def f(slot):
                T.reset_for_write("pool", RK(slot))
                v = ringb[slot][:, :].rearrange("p (kc a n) -> p kc a n", kc=8, a=2)
                T.dma("pool", v[:, :, 0, :], wview(wfi_d[i, j], 256 * t, 256), [], [("ring", slot, 0)])
                T.dma("pool", v[:, :, 1, :], wview(wfi_d[i, j], FF + 256 * t, 256), [], [("ring", slot, 1)])
            return f

        def ld_conv_in(dc):
            def f(slot):
                T.reset_for_write("pool", RK(slot))
                v = ringb[slot][:, 0:3072].rearrange("p (kc a n) -> p kc a n", kc=8, a=3)
                for a in range(3):
                    T.dma("pool", v[:, :, a, :], wview(cwi_d, a * D + dc * 128, 128), [], [("ring", slot, a)])
            return f

        def ld_attn_a(g):
            def f(slot):
                T.reset_for_write("pool", RK(slot))
                v = ringb[slot][:, :].rearrange("p (kc a n) -> p kc a n", kc=8, a=2)
                T.dma("pool", v[:, :, 0, :], wview(awi_d, g * 256, 256), [], [("ring", slot, 0)])
                T.dma("pool", v[:, :, 1, :], wview(awi_d, D + g * 256, 256), [], [("ring", slot, 1)])
            return f

        def ld_attn_b(g):
            def f(slot):
                T.reset_for_write("pool", RK(slot))
                v = ringb[slot][:, 0:8 * 272].rearrange("p (kc n) -> p kc n", kc=8)
                T.dma("pool", v[:, :, 0:256], wview(awi_d, 2 * D + g * 256, 256), [], [("ring", slot, 0)])
                if g == 0:
                    T.dma("pool", v[:, :, 256:272], wview(awi_d, 3 * D, 16), [], [("ring", slot, 1)])
            return f

        def ld_wo(src2d, r0, nfc):
            def f(slot):
                T.reset_for_write("pool", WK)
                h1 = nfc // 2
                for pi, (a, b_) in enumerate(((0, h1), (h1, nfc))):
                    T.dma("pool", Wo[:, a:b_, :], src2d[r0 + a * 128:r0 + b_ * 128, :].rearrange("(f p) n -> p f n", p=128),
                          [], [("wo", pi)])
            return f

        FPARTS = ((0, 12), (12, 10))
        for h in range(2):
            for i in range(2):
                for j in range(2):
                    for (f0, nf) in FPARTS:
                        for t in range(f0 // 2, (f0 + nf) // 2):
                            ring_plan.append(ld_ffn_in(i, j, t))
                        wo_plan.append(ld_wo(wfo_d[i, j], f0 * 128, nf))
                    if j == 0:
                        if i == 0:
                            for dc in range(8):
                                ring_plan.append(ld_conv_in(dc))
                            wo_plan.append(ld_wo(cwo_d, 0, 8))
                        else:
                            for g in range(4):
                                ring_plan.append(ld_attn_a(g))
                                ring_plan.append(ld_attn_b(g))
                            wo_plan.append(ld_wo(awo_d, 0, 8))
        ring = Ring(3, ring_plan)
        wor = Ring(1, wo_plan)

        def tiles(h):
            return [(lt, 128) for lt in range(8)] + ([(8, 64)] if h == 1 else [])

        def groups(h):
            g = [(0, 512, [0, 1, 2, 3]), (512, 512, [4, 5, 6, 7])]
            if h == 1:
                g.append((1024, 64, [8]))
            return g

        tg_of = lambda lt: 2 if lt == 8 else lt // 4

        pending = []
        norm_q = []
        reload_q = []

        def do_reload(a):
            lt, rows, gk_ = a
            T.dma("sp", x[:, lt, :], xp_d[(8 + lt) * 128:(8 + lt + 1) * 128, :], [], [("x", lt)])
            norm_q.append(a)
            while len(norm_q) > 2:
                b_ = norm_q.pop(0)
                flush_pending(keep=2)
                norm_tile(*b_)

        def need_xnT(lt):
            while any(p[0] == lt for p in pending):
                p = pending.pop(0)
                p[1]()

        def flush_pending(keep=0):
            while len(pending) > keep:
                pending.pop(0)[1]()

        st["g"] = 0

        def start_norm(vec_ap):
            k = rr("g", 2)
            T.dma("sp", gtile[k][:], vec_ap.partition_broadcast(128), [], [("gtile", k)])
            return k

        def norm_tile(lt, rows, gk):
            si = rr("ss", 6)
            sm = small[si]
            T.op("act", lambda e: e.activation(out=sq[:rows, :].bitcast(BF16), in_=x[:rows, lt, :], func=AF.Square,
                                               accum_out=sm[:rows, 0:1]), [("x", lt)], ["sq", ("sm", si)])
            T.op("act", lambda e: e.activation(out=sm[:rows, 1:2], in_=sm[:rows, 0:1], func=AF.Ln, scale=1.0 / D,
                                               bias=epsT[:rows, 0:1]), [("sm", si), "eps"], [("sm", si)])
            T.op("act", lambda e: e.activation(out=sm[:rows, 2:3], in_=sm[:rows, 1:2], func=AF.Exp, scale=-0.5),
                 [("sm", si)], [("sm", si)])
            xi = rr("xns", 7)
            if xi < 3:
                xb = xns[xi]; xks = [("xns", xi)]
            elif xi < 5:
                xb = qTr[xi - 3][:, 0:1024]; xks = [("qTr", xi - 3, j) for j in range(3)]
            else:
                o4 = (xi - 5) * 4
                xb = Otok[:, o4:o4 + 4, :].rearrange("p t c -> p (t c)"); xks = [("Otok", o4 + j) for j in range(4)]
            T.op("dve", lambda e: e.scalar_tensor_tensor(out=xb[:rows, :], in0=x[:rows, lt, :], scalar=sm[:rows, 2:3],
                                                         in1=gtile[gk][:rows, :], op0=ALU.mult, op1=ALU.mult),
                 [("x", lt), ("sm", si), ("gtile", gk)], xks)

            def stage_b():
                b = pbank()
                psb = ps[b][:, :].bitcast(BF16)

                def fn(pe):
                    for kc in range(8):
                        ins = pe.transpose(psb[:, kc * 128:kc * 128 + rows], xb[:rows, kc * 128:(kc + 1) * 128],
                                           ident[:rows, :rows])
                    return ins
                T.op("pe", fn, xks + ["ident"], [("ps", b)])
                copy_op(eveng(), xnT[:, :, lt * 128:lt * 128 + rows],
                        psb.rearrange("p (k n) -> p k n", n=128)[:, :, 0:rows], [("ps", b)], [("xnT", lt)])
            pending.append((lt, stage_b))

        def after_tile(post, lt, rows):
            if post is None:
                return
            if post[0] == "norm":
                flush_pending(keep=2)
                norm_tile(lt, rows, post[1])
            else:
                h = post[1]
                if lt < 8:
                    T.dma("sp", yp_d[(h * 8 + lt) * 128:(h * 8 + lt + 1) * 128, :], x[:, lt, :], [("x", lt)], [], is_out=True)
                    if h == 0:
                        reload_q.append((lt, rows, post[2]))
                        while len(reload_q) > 1:
                            do_reload(reload_q.pop(0))
                else:
                    T.dma("sp", ys_d[:, :], x[:64, lt, :], [("x", lt)], [], is_out=True)

        def ffn(h, post):
            gv = gTv()
            for pi, (f0, nf) in enumerate(FPARTS):
                for t in range(nf // 2):
                    slot = ring.acquire()
                    wv = ringb[slot][:, :].rearrange("p (kc a n) -> p kc a n", kc=8, a=2)
                    for (c0, n, tl) in groups(h):
                        for lt in tl:
                            need_xnT(lt)
                        xk = [("xnT", lt) for lt in tl]
                        for sub in range(2):
                            fcl = 2 * t + sub
                            bg = pbank(); bu = pbank()
                            mm_group(ps[bg][:, 0:n], [(wv[:, kc, 0, sub * 128:(sub + 1) * 128], xnT[:, kc, c0:c0 + n]) for kc in range(8)],
                                     RK(slot) + xk, [("ps", bg)])
                            mm_group(ps[bu][:, 0:n], [(wv[:, kc, 1, sub * 128:(sub + 1) * 128], xnT[:, kc, c0:c0 + n]) for kc in range(8)],
                                     RK(slot) + xk, [("ps", bu)])
                            si = rr("stmp", 3)
                            T.op("act", lambda e: e.activation(out=stmp[si][:, 0:n], in_=ps[bg][:, 0:n], func=AF.Silu),
                                 [("ps", bg)], [("stmp", si)])
                            T.op("dve", lambda e: e.tensor_tensor(out=gv[:, fcl, c0:c0 + n], in0=stmp[si][:, 0:n], in1=ps[bu][:, 0:n],
                                                                  op=ALU.mult), [("stmp", si), ("ps", bu)], [("gT", fcl, c0)])
                    ring.release()
                    if prologue_q:
                        prologue_q.pop(0)()
                wor.acquire()
                for (lt, rows) in tiles(h):
                    c0g = groups(h)[tg_of(lt)][0]
                    for ch in range(2):
                        b = pbank()
                        mm_group(ps[b][:rows, :], [(gv[:, fcl, lt * 128:lt * 128 + rows], Wo[:, fcl, ch * 512:(ch + 1) * 512]) for fcl in range(nf)],
                                 WK + [("gT", fcl, c0g) for fcl in range(nf)], [("ps", b)])
                        T.op("dve", lambda e: e.scalar_tensor_tensor(out=x[:rows, lt, ch * 512:(ch + 1) * 512], in0=ps[b][:rows, :], scalar=0.5,
                                                                     in1=x[:rows, lt, ch * 512:(ch + 1) * 512], op0=ALU.mult, op1=ALU.add),
                             [("ps", b), ("x", lt)], [("x", lt)])
                    if pi == 1:
                        after_tile(post, lt, rows)
                wor.release()

        def out_proj(h, post):
            wor.acquire()
            for (lt, rows) in tiles(h):
                for ch in range(2):
                    b = pbank()
                    mm_group(ps[b][:rows, :], [(mT[:, dc, lt * 128:lt * 128 + rows], Wo[:, dc, ch * 512:(ch + 1) * 512]) for dc in range(8)],
                             WK + [("mT", lt)], [("ps", b)])
                    T.op("dve", lambda e: e.tensor_tensor(out=x[:rows, lt, ch * 512:(ch + 1) * 512], in0=ps[b][:rows, :],
                                                          in1=x[:rows, lt, ch * 512:(ch + 1) * 512], op=ALU.add),
                         [("ps", b), ("x", lt)], [("x", lt)])
                after_tile(post, lt, rows)
            wor.release()

        WKK = [("wk", 0), ("wk", 1), ("wk", 2)]

        def conv(h):
            for dc in range(8):
                slot = ring.acquire()
                wv = ringb[slot][:, 0:3072].rearrange("p (kc a n) -> p kc a n", kc=8, a=3)
                prev_ub = None
                gl = list(enumerate(groups(h)))
                if h == 1:
                    gl = [gl[2], gl[0], gl[1]]
                for gi, (c0, n, tl) in gl:
                    for lt in tl:
                        need_xnT(lt)
                    xk = [("xnT", lt) for lt in tl]
                    bb = [pbank() for _ in range(3)]
                    for a in range(3):
                        mm_group(ps[bb[a]][:, 0:n], [(wv[:, kc, a, :], xnT[:, kc, c0:c0 + n]) for kc in range(8)],
                                 RK(slot) + xk, [("ps", bb[a])])
                    ui = rr("ub", 2)
                    u = ub[ui]
                    if gi == 2:
                        copy_op("act", u[:, 0:2], hsin[:, :, dc], [("hsin", 0), ("hsin", 1)], [("ub", ui)])
                    elif gi == 0:
                        copy_op("act", u[:, 0:2], histp[:, :, dc], ["histp"], [("ub", ui)])
                    else:
                        copy_op("act", u[:, 0:2], prev_ub[0][:, prev_ub[1]:prev_ub[1] + 2], [("ub", prev_ub[2])], [("ub", ui)])
                    si = rr("stmp", 3)
                    copy_op("act", stmp[si][:, 0:n], ps[bb[1]][:, 0:n], [("ps", bb[1])], [("stmp", si)])
                    T.op("dve", lambda e: e.tensor_tensor(out=u[:, 2:2 + n], in0=stmp[si][:, 0:n], in1=ps[bb[2]][:, 0:n], op=ALU.mult),
                         [("stmp", si), ("ps", bb[2])], [("ub", ui)])
                    tAb, tAk = ((tA, "tA"), (sq, "sq"))[rr("ta", 2)]
                    T.op("act", lambda e: e.activation(out=tAb[:, 0:n], in_=u[:, 0:n], func=AF.Copy, scale=wk[:, 0, dc:dc + 1]),
                         [("ub", ui)] + WKK, [tAk])
                    T.op("dve", lambda e: e.scalar_tensor_tensor(out=tAb[:, 0:n], in0=u[:, 1:1 + n], scalar=wk[:, 1, dc:dc + 1], in1=tAb[:, 0:n],
                                                                 op0=ALU.mult, op1=ALU.add), [("ub", ui), tAk] + WKK, [tAk])
                    T.op("dve", lambda e: e.scalar_tensor_tensor(out=tAb[:, 0:n], in0=u[:, 2:2 + n], scalar=wk[:, 2, dc:dc + 1], in1=tAb[:, 0:n],
                                                                 op0=ALU.mult, op1=ALU.add), [("ub", ui), tAk] + WKK, [tAk])
                    T.op("dve", lambda e: e.tensor_tensor(out=mT[:, dc, c0:c0 + n], in0=tAb[:, 0:n], in1=ps[bb[0]][:, 0:n], op=ALU.mult),
                         [tAk, ("ps", bb[0])], [("mT", lt) for lt in tl])
                    if gi == 1:
                        copy_op("act", histp[:, :, dc], u[:, n:n + 2], [("ub", ui)], ["histp"])
                    if gi == 2:
                        copy_op("act", hsout[:, :, dc], u[:, n:n + 2], [("ub", ui)], ["hsout"])
                    if gi != 2:
                        prev_ub = (u, n, ui)
                ring.release()

        def cumsum_tile(lf_ap, rows, Rt, Rkey, c_ap, ckey, lfkeys):
            b = pbank()

            def fn(pe):
                pe.matmul(ps[b][:rows, 0:16], U[:rows, :rows], lf_ap, start=True, stop=False)
                return pe.matmul(ps[b][:rows, 0:16], ones[:, :rows], Rt[:, :], start=False, stop=True)
            T.op("pe", fn, lfkeys + [Rkey, "U", "ones"], [("ps", b)])
            T.op("dve", lambda e: e.tensor_copy(out=c_ap, in_=ps[b][:rows, 0:16]), [("ps", b)], [ckey])
            T.op("dve", lambda e: e.tensor_tensor(out=Rt[:rows, :], in0=Rt[:rows, :], in1=lf_ap, op=ALU.add), lfkeys + [Rkey], [Rkey])

        def cumsum_post(rows, nt_, c_v, nb_v, ckeys, cs_v=None):
            T.op("dve", lambda e: e.scalar_tensor_tensor(out=nb_v, in0=c_v, scalar=-1.0,
                                                         in1=negM16[:rows, :].unsqueeze(1).to_broadcast([rows, nt_, 16]), op0=ALU.mult, op1=ALU.add),
                 ckeys + ["negM16"], [(k, "nb") for k in ckeys])
            if cs_v is not None:
                si = rr("stmp", 3)
                n_ = nt_ * 16
                r1 = stmp[si][:rows, 0:n_].rearrange("p (t h) -> p t h", h=16)
                r2 = stmp[si][:rows, 128:128 + n_].rearrange("p (t h) -> p t h", h=16)
                k = ("stmp", si)
                T.op("dve", lambda e: e.tensor_copy(out=cs_v[:, :, :, 0], in_=c_v), ckeys, [(kk, "cs") for kk in ckeys])
                T.op("dve", lambda e: e.tensor_tensor(out=r1, in0=c_v, in1=cs_v[:, :, :, 0], op=ALU.subtract), ckeys + [(kk, "cs") for kk in ckeys], [k])
                T.op("dve", lambda e: e.tensor_copy(out=cs_v[:, :, :, 1], in_=r1), [k], [(kk, "cs1") for kk in ckeys])
                T.op("dve", lambda e: e.tensor_tensor(out=r2, in0=r1, in1=cs_v[:, :, :, 1], op=ALU.subtract), [k] + [(kk, "cs1") for kk in ckeys], [(k, 2)])
                T.op("dve", lambda e: e.tensor_copy(out=cs_v[:, :, :, 2], in_=r2), [(k, 2), k], [(kk, "cs2") for kk in ckeys])

        def attn_job(g, q_tiles, key_tiles, nb, cname, batch=False):
            Kv = Kg(); Vv = Vg(); Qv = Qg()
            nk = len(key_tiles)
            LA = 3
            slots = {}
            PTs = PT + [tA[:, :].bitcast(BF16)]
            PTk = [("PT", 0), ("PT", 1), ("PT", 2), "tA"]

            def emit_transposes(hh):
                ks = rr("ktr", 2); qs = rr("qtr", 2)
                slots[hh] = (ks, qs)
                for b0 in range(0, nk, 4):
                    chunk = key_tiles[b0:b0 + 4]
                    b = pbank()
                    psb = ps[b][:, :].bitcast(BF16)

                    def fn(pe, chunk=chunk, psb=psb):
                        for j, (idx, rows, _, _) in enumerate(chunk):
                            ins = pe.transpose(psb[0:64, j * 128:j * 128 + rows], Kv[:rows, idx, hh * 64:(hh + 1) * 64], ident[:rows, :rows])
                        return ins
                    T.op("pe", fn, [("Kg", idx) for (idx, _, _, _) in chunk] + ["ident"], [("ps", b)])
                    ncol = (len(chunk) - 1) * 128 + chunk[-1][1]
                    copy_op("dve", kTr[ks][0:64, b0 * 128:b0 * 128 + ncol], psb[0:64, 0:ncol], [("ps", b)], [("kTr", ks, b0 // 4)])
                for b0 in range(0, len(q_tiles), 4):
                    chunk = q_tiles[b0:b0 + 4]
                    b = pbank()
                    psb = ps[b][:, :].bitcast(BF16)

                    def fn(pe, chunk=chunk, psb=psb):
                        for j, (lt, rows) in enumerate(chunk):
                            ins = pe.transpose(psb[0:67, j * 128:j * 128 + rows], Qv[:rows, lt, hh, 0:67], ident[:rows, :rows])
                        return ins
                    T.op("pe", fn, [("Qg", lt) for (lt, _) in chunk] + ["ident"], [("ps", b)])
                    ncol = (len(chunk) - 1) * 128 + chunk[-1][1]
                    copy_op("dve", qTr[qs][0:67, b0 * 128:b0 * 128 + ncol], psb[0:67, 0:ncol], [("ps", b)], [("qTr", qs, b0 // 4)])

            steps = []
            for hh in range(4):
                for q0 in range(0, len(q_tiles), 4):
                    qt = q_tiles[q0:q0 + 4]
                    last_pos = q0 + len(qt) - 1
                    rel = [kt for kt in key_tiles if kt[3] is None or kt[3] <= last_pos]
                    grp = {"ob": None}
                    if batch:
                        pk_ = [kt for kt in rel if kt[3] is None]
                        ow_ = [kt for kt in rel if kt[3] is not None]
                        ki_ = 0
                        for b0 in range(0, len(pk_), 8):
                            steps.append({"hh": hh, "q0": q0, "qt": qt, "ki": ki_, "kt": pk_[b0], "batch": pk_[b0:b0 + 8], "last": False,
                                          "grp": grp, "newhead": (q0 == 0 and ki_ == 0)})
                            ki_ += 1
                        for kt in ow_:
                            steps.append({"hh": hh, "q0": q0, "qt": qt, "ki": ki_, "kt": kt, "batch": None, "last": kt is ow_[-1], "grp": grp,
                                          "newhead": False})
                            ki_ += 1
                        continue
                    for ki_, kt in enumerate(rel):
                        steps.append({"hh": hh, "q0": q0, "qt": qt, "ki": ki_, "kt": kt, "batch": None, "last": ki_ == len(rel) - 1, "grp": grp,
                                      "newhead": (q0 == 0 and ki_ == 0)})

            def front(s):
                hh = s["hh"]; q0 = s["q0"]; qt = s["qt"]
                ks, qs = slots[hh]
                head = g * 4 + hh
                if s["batch"]:
                    nq = sum(r for _, r in qt)
                    b = pbank()
                    bl = s["batch"]

                    def fnq(pe):
                        for j, kt in enumerate(bl):
                            kcol = key_tiles.index(kt) * 128
                            ins = pe.matmul(ps[b][:, j * nq:(j + 1) * nq], kTr[ks][0:67, kcol:kcol + 128],
                                            qTr[qs][0:67, q0 * 128:q0 * 128 + nq], start=True, stop=True)
                        return ins
                    T.op("pe", fnq, [("kTr", ks, key_tiles.index(kt) // 4) for kt in bl] + [("qTr", qs, q0 // 4)], [("ps", b)])
                    pi = rr("pt", 4)
                    T.op("act", lambda e: e.activation(out=PTs[pi][:, 0:len(bl) * nq], in_=ps[b][:, 0:len(bl) * nq], func=AF.Exp,
                                                       bias=negM16[:, 0:1], scale=1.0), [("ps", b), "negM16"], [PTk[pi]])
                    s["pi"] = pi; s["s_t"] = 0
                    return
                idx, krows, nbidx, own_pos = s["kt"]
                nq = sum(r for _, r in qt)
                kcol = key_tiles.index(s["kt"]) * 128
                s_t = 0 if (own_pos is None or own_pos < q0) else own_pos - q0
                start_c = s_t * 128
                N = nq - start_c
                b = pbank()
                T.op("pe", lambda pe: pe.matmul(ps[b][:krows, 0:N], kTr[ks][0:67, kcol:kcol + krows],
                                                qTr[qs][0:67, q0 * 128 + start_c:q0 * 128 + nq], start=True, stop=True),
                     [("kTr", ks, kcol // 512), ("qTr", qs, q0 // 4)], [("ps", b)])
                pi = rr("pt", 4)
                T.op("act", lambda e: e.activation(out=PTs[pi][:krows, 0:N], in_=ps[b][:krows, 0:N], func=AF.Exp,
                                                   bias=nb[:krows, nbidx, head:head + 1], scale=1.0),
                     [("ps", b), ((cname, nbidx), "nb")], [PTk[pi]])
                if own_pos is not None and own_pos >= q0:
                    dq = qt[s_t][1]
                    T.op("dve", lambda e: e.tensor_tensor(out=PTs[pi][:krows, 0:dq], in0=PTs[pi][:krows, 0:dq], in1=tri[:krows, 0:dq],
                                                          op=ALU.mult), [PTk[pi], "tri"], [PTk[pi]])
                s["pi"] = pi; s["s_t"] = s_t

            def back(s):
                hh = s["hh"]; q0 = s["q0"]; qt = s["qt"]; ki_ = s["ki"]; pi = s["pi"]; s_t = s["s_t"]
                idx, krows, nbidx, own_pos = s["kt"]
                if s["grp"]["ob"] is None:
                    s["grp"]["ob"] = 6 + rr("ob", 2)
                ob = s["grp"]["ob"]

                if s["batch"]:
                    bl = s["batch"]
                    nq = sum(r for _, r in qt)

                    def fnb(pe):
                        for j, kt in enumerate(bl):
                            ins = pe.matmul(ps[ob][:nq, 0:65], PTs[pi][:, j * nq:(j + 1) * nq], Vv[:, kt[0], hh, 0:65],
                                            start=(ki_ == 0 and j == 0), stop=False, skip_group_check=True)
                        return ins
                    T.op("pe", fnb, [PTk[pi]] + [("Vg", kt[0], hh) for kt in bl], [("ps", ob)])
                    return

                def fn(pe):
                    for i in range(s_t, len(qt)):
                        qrows = qt[i][1]
                        first = (ki_ == 0 and i == 0)
                        lastk = (own_pos is not None and own_pos == q0 + i)
                        ins = pe.matmul(ps[ob][:qrows, i * 65:i * 65 + 65], PTs[pi][:krows, (i - s_t) * 128:(i - s_t) * 128 + qrows],
                                        Vv[:krows, idx, hh, 0:65], start=first, stop=lastk, skip_group_check=True)
                    return ins
                T.op("pe", fn, [PTk[pi], ("Vg", idx, hh)], [("ps", ob)])
                if s["last"]:
                    nt_ = len(qt)
                    qrows = qt[0][1]
                    lt0 = qt[0][0]
                    si = rr("ss", 6)
                    ov = ps[ob][:qrows, 0:nt_ * 65].rearrange("p (t c) -> p t c", c=65)
                    T.op("dve", lambda e: e.reciprocal(out=small[si][:qrows, 0:nt_], in_=ov[:, :, 64]), [("ps", ob)], [("sm", si)])
                    T.op("dve", lambda e: e.tensor_tensor(out=Otok[:qrows, lt0:lt0 + nt_, hh * 64:(hh + 1) * 64], in0=ov[:, :, 0:64],
                                                          in1=small[si][:qrows, 0:nt_].unsqueeze(2).to_broadcast([qrows, nt_, 64]), op=ALU.mult),
                         [("ps", ob), ("sm", si)], [("Otok", lt) for (lt, _) in qt])

            emit_transposes(0)
            for i in range(len(steps) + LA):
                if i < len(steps):
                    s = steps[i]
                    front(s)
                    if s["newhead"] and s["hh"] + 1 < 4:
                        emit_transposes(s["hh"] + 1)
                if i >= LA:
                    back(steps[i - LA])
            for (lt, rows) in q_tiles:
                b = pbank()
                psb = ps[b][:, :].bitcast(BF16)

                def fn(pe):
                    for j in range(2):
                        ins = pe.transpose(psb[:, j * 128:j * 128 + rows], Otok[:rows, lt, j * 128:(j + 1) * 128], ident[:rows, :rows])
                    return ins
                T.op("pe", fn, [("Otok", lt), "ident"], [("ps", b)])
                copy_op(eveng(), mT[:, 2 * g:2 * g + 2, lt * 128:lt * 128 + rows],
                        psb[:, 0:256].rearrange("p (k n) -> p k n", n=128)[:, :, 0:rows], [("ps", b)], [("mT", lt)])

        def attention(h):
            Kv = Kg(); Vv = Vg(); Qv = Qg()
            npast = 8 * h
            st["npb"] = 6
            cast_eng = "dve" if h == 1 else "act"
            T.op("dve", lambda e: e.memset(Vv[:, :, :, 64:65], 1.0), [],
                 ["gTfree"] + [("Vg", i, hh) for i in range(17) for hh in range(4)] + [("Kg", i) for i in range(17)] + [("Qg", i) for i in range(9)]
                 + [("gT", f, c) for f in range(12) for c in (0, 512, 1024)])
            for g in range(4):
                if h == 1:
                    T.op("dve", lambda e: e.memset(Vv[:, 0:16, :, 64:65], 1.0), ["gTfree"], [("Vg", i, hh) for i in range(16) for hh in range(4)])
                    T.dma("pool", Kv[:, 0:8, :], kp_d[0:1024, g * 256:(g + 1) * 256].rearrange("(t p) c -> p t c", p=128),
                          ["gTfree"] + [("kp_out", g, i) for i in range(8)], [("Kg", i) for i in range(8)])
                    for hh in range(4):
                        T.dma("pool", Vv[:, 0:8, hh, 0:64], vp_d[0:1024, g * 256 + hh * 64:g * 256 + (hh + 1) * 64].rearrange("(t p) c -> p t c", p=128),
                              ["gTfree"] + [("vp_out", g, i) for i in range(8)], [("Vg", i, hh) for i in range(8)])
                sa = ring.acquire(0)
                wa = ringb[sa][:, :].rearrange("p (kc n) -> p kc n", kc=8)
                sb_ = ring.acquire(1)
                wb = ringb[sb_][:, 0:8 * 272].rearrange("p (kc n) -> p kc n", kc=8)
                nB = 272 if g == 0 else 256
                st["npb"] = 8
                tails = []
                st2 = []
                for (lt, rows) in tiles(h):
                    need_xnT(lt)
                    is_s = (lt == 8)
                    kidx = 16 if is_s else npast + lt
                    bA = pbank(); bB = pbank()
                    mm_group(ps[bA][:rows, 0:512], [(xnT[:, kc, lt * 128:lt * 128 + rows], wa[:, kc, :]) for kc in range(8)],
                             RK(sa) + [("xnT", lt)], [("ps", bA)])
                    mm_group(ps[bB][:rows, 0:nB], [(xnT[:, kc, lt * 128:lt * 128 + rows], wb[:, kc, 0:nB]) for kc in range(8)],
                             RK(sb_) + [("xnT", lt)], [("ps", bB)])
                    T.op("act", lambda e: e.activation(out=sq[:rows, :], in_=ps[bA][:rows, :], func=AF.Square), [("ps", bA)], ["sq"])
                    si = rr("ss", 6)
                    sm = small[si]
                    T.op("dve", lambda e: e.reduce_sum(out=sm[:rows, 0:8], in_=sq[:rows, :].rearrange("p (h c) -> p h c", c=64), axis=AX.X),
                         ["sq"], [("sm", si)])
                    T.op("act", lambda e: e.activation(out=sm[:rows, 8:16], in_=sm[:rows, 0:8], func=AF.Ln, scale=1.0 / HD, bias=epsT[:rows, 0:1]),
                         [("sm", si), "eps"], [("sm", si)])
                    T.op("act", lambda e: e.activation(out=sm[:rows, 16:24], in_=sm[:rows, 8:16], func=AF.Exp, scale=-0.5), [("sm", si)], [("sm", si)])
                    def stage2(lt=lt, rows=rows, is_s=is_s, kidx=kidx, bA=bA, bB=bB, si=si, sm=sm):
                        t1 = rr("stmp", 3)
                        T.op("dve", lambda e: e.tensor_tensor(out=stmp[t1][:rows, 0:256].rearrange("p (h c) -> p h c", c=64),
                                                              in0=ps[bA][:rows, 0:256].rearrange("p (h c) -> p h c", c=64),
                                                              in1=sm[:rows, 16:20].unsqueeze(2).to_broadcast([rows, 4, 64]), op=ALU.mult),
                             [("ps", bA), ("sm", si)], [("stmp", t1)])
                        T.op("dve", lambda e: e.tensor_tensor(out=Qv[:rows, lt, :, 0:64], in0=stmp[t1][:rows, 0:256].rearrange("p (h c) -> p h c", c=64),
                                                              in1=gq8[:rows, :].unsqueeze(1).to_broadcast([rows, 4, 64]), op=ALU.mult),
                             [("stmp", t1), "gq8"], [("Qg", lt)])
                        t2 = rr("stmp", 3)
                        T.op("dve", lambda e: e.tensor_tensor(out=stmp[t2][:rows, 0:256].rearrange("p (h c) -> p h c", c=64),
                                                              in0=ps[bA][:rows, 256:512].rearrange("p (h c) -> p h c", c=64),
                                                              in1=sm[:rows, 20:24].unsqueeze(2).to_broadcast([rows, 4, 64]), op=ALU.mult),
                             [("ps", bA), ("sm", si)], [("stmp", t2)])
                        ki = rr("kst", 3)
                        kb, kk = (kst[ki], ("kst", ki)) if ki < 2 else (ub[0][:, 0:256], ("ub", 0))
                        vb, vk = (vst[ki], ("vst", ki)) if ki < 2 else (ub[1][:, 0:256], ("ub", 1))
                        T.op("dve", lambda e: e.tensor_tensor(out=kb[:rows, :].rearrange("p (h c) -> p h c", c=64),
                                                              in0=stmp[t2][:rows, 0:256].rearrange("p (h c) -> p h c", c=64),
                                                              in1=gkT[:rows, :].unsqueeze(1).to_broadcast([rows, 4, 64]), op=ALU.mult),
                             [("stmp", t2), "gkT"], [kk])
                        copy_op(cast_eng, Kv[:rows, kidx, :], kb[:rows, :], [kk], [("Kg", kidx)])
                        copy_op("act", vb[:rows, :], ps[bB][:rows, 0:256], [("ps", bB)], [vk])
                        copy_op(cast_eng, Vv[:rows, kidx, :, 0:64], ps[bB][:rows, 0:256].rearrange("p (h c) -> p h c", c=64),
                                [("ps", bB)], [("Vg", kidx, hh) for hh in range(4)])
                        if is_s:
                            T.dma("sp", ks_d[:, g * 256:(g + 1) * 256], kb[:rows, :], [kk], [], is_out=True)
                            T.dma("sp", vs_d[:, g * 256:(g + 1) * 256], vb[:rows, :], [vk], [], is_out=True)
                        else:
                            r0 = (h * 8 + lt) * 128
                            T.dma("sp", kp_d[r0:r0 + 128, g * 256:(g + 1) * 256], kb[:, :], [kk], [("kp_out", g, lt)] if h == 0 else [], is_out=True)
                            T.dma("sp", vp_d[r0:r0 + 128, g * 256:(g + 1) * 256], vb[:, :], [vk], [("vp_out", g, lt)] if h == 0 else [], is_out=True)
                        if g == 0:
                            li = rr("lf", 5)
                            lf = lfst[li]
                            s2 = rr("ss", 6)
                            T.op("dve", lambda e: e.tensor_tensor(out=small[s2][:rows, 0:16], in0=ps[bB][:rows, 256:272], in1=bfT[:rows, :], op=ALU.add),
                                 [("ps", bB), "bfT"], [("sm", s2)])
                            T.op("act", lambda e: e.activation(out=small[s2][:rows, 16:32], in_=small[s2][:rows, 0:16], func=AF.Exp, scale=-1.0),
                                 [("sm", s2)], [("sm", s2)])
                            T.op("act", lambda e: e.activation(out=small[s2][:rows, 32:48], in_=small[s2][:rows, 16:32], func=AF.Ln, bias=oneT[:rows, 0:1]),
                                 [("sm", s2), "oneT"], [("sm", s2)])
                            T.op("dve", lambda e: e.tensor_scalar(out=lf[:rows, :], in0=small[s2][:rows, 32:48], scalar1=-1.0, scalar2=None, op0=ALU.mult),
                                 [("sm", s2)], [("lf", li)])
                            if is_s:
                                T.dma("sp", lfs_d[:, :], lf[:rows, :], [("lf", li)], [], is_out=True)
                            else:
                                T.dma("sp", lfp_d[(h * 8 + lt) * 128:(h * 8 + lt + 1) * 128, :], lf[:, :], [("lf", li)], [], is_out=True)

                        def tail(lt=lt, rows=rows, is_s=is_s, li=(li if g == 0 else None)):
                            if g == 0:
                                if is_s:
                                    cumsum_tile(lfst[li][:rows, :], rows, R_s, "R_s", c_s[:rows, 16, :], ("c_s", 16), [("lf", li)])
                                else:
                                    gt_ = h * 8 + lt
                                    cumsum_tile(lfst[li][:rows, :], rows, R_p, "R_p", c_p[:rows, gt_, :], ("c_p", gt_), [("lf", li)])
                        tails.append(tail)
                        while len(tails) > 3:
                            tails.pop(0)()
                    st2.append(stage2)
                    while len(st2) > 1:
                        st2.pop(0)()
                while st2:
                    st2.pop(0)()
                while tails:
                    tails.pop(0)()
                pk = [("c_p", h * 8 + i) for i in range(8)]
                if g == 0:
                    cumsum_post(128, 8, c_p[:, h * 8:h * 8 + 8, :], nb_p[:, h * 8:h * 8 + 8, :], pk, cs_v=cs_p[:, h * 8:h * 8 + 8, :, :])
                    if h == 1:
                        cumsum_post(64, 1, c_s[:64, 16:17, :], nb_s[:64, 16:17, :], [("c_s", 16)], cs_v=cs_s[:64, 0:1, :, :])
                T.op("dve", lambda e: e.tensor_copy(out=Qv[:, 0:8, :, 64:67], in_=cs_p[:, h * 8:h * 8 + 8, g * 4:(g + 1) * 4, :]),
                     [(k, c) for k in pk for c in ("cs", "cs1", "cs2")], [("Qg", lt) for lt in range(8)])
                if h == 1:
                    T.op("dve", lambda e: e.tensor_copy(out=Qv[:64, 8, :, 64:67], in_=cs_s[:64, 0, g * 4:(g + 1) * 4, :]),
                         [(("c_s", 16), c) for c in ("cs", "cs1", "cs2")], [("Qg", 8)])
                ring.release()
                ring.release()
                st["npb"] = 6
                ktl = [(i, 128, i, None) for i in range(npast)] + [(npast + i, 128, npast + i, i) for i in range(8)]
                attn_job(g, [(lt, 128) for lt in range(8)], ktl, nb_p, "c_p")
                if h == 1:
                    T.dma("pool", Kv[:, 0:16, :], ck_d[:, g * 256:(g + 1) * 256].rearrange("(t p) c -> p t c", p=128),
                          [], [("Kg", i) for i in range(16)])
                    for hh in range(4):
                        T.dma("pool", Vv[:, 0:16, hh, 0:64], cv_d[:, g * 256 + hh * 64:g * 256 + (hh + 1) * 64].rearrange("(t p) c -> p t c", p=128),
                              [], [("Vg", i, hh) for i in range(16)])
                    T.op("dve", lambda e: e.tensor_tensor(out=Vv[:, 0:16, :, 0:65], in0=Vv[:, 0:16, :, 0:65],
                                                          in1=nb_s[:, 0:16, g * 4:(g + 1) * 4].unsqueeze(3).to_broadcast([128, 16, 4, 65]), op=ALU.mult),
                         ["w_s"] + [("Vg", i, hh) for i in range(16) for hh in range(4)], [("Vg", i, hh) for i in range(16) for hh in range(4)])
                    ktl = [(i, 128, i, None) for i in range(16)] + [(16, 64, 16, 0)]
                    attn_job(g, [(8, 64)], ktl, nb_s, "c_s", batch=True)

        for lt in range(4):
            T.dma("sp", x[:, lt, :], xp_d[lt * 128:(lt + 1) * 128, :], [], [("x", lt)])
        NV = [nffn_d[0, 0], nmix_d[0], nffn_d[0, 1], nffn_d[1, 0], nmix_d[1], nffn_d[1, 1]] * 2
        gks = {}

        def load_norm(i):
            if i < len(NV):
                gks[i] = start_norm(NV[i])
        load_norm(0)
        gk0 = gks[0]
        T._waits("sp", [("x", lt) for lt in range(4)], [])
        for lt in range(4, 8):
            T.dma("sp", x[:, lt, :], xp_d[lt * 128:(lt + 1) * 128, :], [], [("x", lt)])
        T.dma("sp", x[:64, 8, :], xs_d[:, :], [], [("x", 8)])
        load_norm(1)
        for n_ in (1, 2, 3):
            ring.n = n_
            ring.prefetch()
            if n_ < 3:
                T._waits("pool", RK(n_ - 1), [])
        wor.prefetch()
        consts_late()
        for t in range(16):
            prologue_q.append(lambda t=t: cumsum_tile(lfc[:, t, :], 128, R_s, "R_s", c_s[:, t, :], ("c_s", t), ["lfc"]))

        def prologue_tail():
            bce = pbank()
            T.op("pe", lambda pe: pe.matmul(ps[bce][:, 0:16], ones[:, :], R_s[:, :], start=True, stop=True), ["R_s", "ones"], [("ps", bce)])
            T.op("dve", lambda e: e.tensor_tensor(out=nb_s[:, 0:16, :], in0=c_s[:, 0:16, :], in1=ps[bce][:, 0:16].unsqueeze(1).to_broadcast([128, 16, 16]),
                                                  op=ALU.subtract), [("ps", bce)] + [("c_s", t) for t in range(16)], ["w_s"])
            T.op("act", lambda e: e.activation(out=nb_s[:, 0:16, :], in_=nb_s[:, 0:16, :], func=AF.Exp, scale=-1.0), ["w_s"], ["w_s"])
            T.op("dve", lambda e: e.memset(R_s[:], 0.0), [], ["R_s"])
        prologue_q.append(prologue_tail)

        for h in range(2):
            if h == 0:
                gk = gk0
                for (lt, rows) in tiles(h):
                    norm_tile(lt, rows, gk)
                    flush_pending(keep=2)
            else:
                norm_tile(8, 64, gk_next)
            base = 6 * h
            load_norm(base + 2)
            ffn(h, ("norm", gks[base + 1]))
            load_norm(base + 3)
            conv(h)
            out_proj(h, ("norm", gks[base + 2]))
            load_norm(base + 4)
            if h == 1:
                for t in range(2):
                    T.dma("sp", csp_d[t].rearrange("(dc p) -> p dc", p=128), histp[:, t, :], ["histp"], [], is_out=True,
                          allow_slow_non_contiguous=True)
                    T.dma("sp", css_d[t].rearrange("(dc p) -> p dc", p=128), hsout[:, t, :], ["hsout"], [], is_out=True,
                          allow_slow_non_contiguous=True)
            ffn(h, ("norm", gks[base + 3]))
            load_norm(base + 5)
            ffn(h, ("norm", gks[base + 4]))
            flush_pending()
            while prologue_q:
                prologue_q.pop(0)()
            load_norm(base + 6)
            attention(h)
            st["npb"] = 8
            out_proj(h, ("norm", gks[base + 5]))
            load_norm(base + 7)
            gk_next = gks.get(base + 6)
            ffn(h, ("out", h, gk_next))
            while reload_q:
                do_reload(reload_q.pop(0))
            while norm_q:
                a = norm_q.pop(0)
                flush_pending(keep=2)
                norm_tile(*a)
        T.finish()
    return nc


_NC = None


def kernel(x_prompt, x_sample, state_conv, cache_k, cache_v, cache_logf, norm_ffn, ffn_w_in, ffn_w_out, norm_mix,
           conv_w_in, conv_w, conv_w_out, attn_w_in, attn_b_f, q_norm, k_norm, attn_w_out):
    global _NC
    if _NC is None:
        _NC = build_nc()
    nc = _NC
    f = lambda a: np.ascontiguousarray(np.asarray(a, dtype=np.float32))
    shared = {
        "norm_ffn": f(norm_ffn), "ffn_w_in": f(ffn_w_in), "ffn_w_out": f(ffn_w_out), "norm_mix": f(norm_mix),
        "conv_w_in": f(conv_w_in)[0], "conv_w": f(conv_w)[0], "conv_w_out": f(conv_w_out)[0],
        "attn_w_in": f(attn_w_in)[0], "attn_b_f": f(attn_b_f)[0], "q_norm": f(q_norm)[0], "k_norm": f(k_norm)[0],
        "attn_w_out": f(attn_w_out)[0],
    }
    xp = f(x_prompt); xs = f(x_sample); sc = f(state_conv); ck = f(cache_k); cv = f(cache_v); cl = f(cache_logf)
    in_maps = []
    for b in range(NCORES):
        m = dict(shared)
        m.update({"xp": xp[b], "xs": xs[b], "sconv": sc[0, b], "ck": ck[0, b].reshape(PL, D), "cv": cv[0, b].reshape(PL, D),
                  "clf": cl[0, b]})
        in_maps.append(m)
    res = run_bass_kernel_spmd(nc, in_maps, core_ids=list(range(NCORES)))
    R = res.results
    st = lambda k: np.stack([np.asarray(R[b][k], dtype=np.float32) for b in range(NCORES)])
    yp = st("yp"); ys = st("ys")
    csp = st("csp")[None]; css = st("css")[None]
    kp = st("kp").reshape(1, NCORES, PL, H, HD); vp = st("vp").reshape(1, NCORES, PL, H, HD); lfp = st("lfp")[None]
    ks = st("ks").reshape(1, NCORES, SL, H, HD); vs = st("vs").reshape(1, NCORES, SL, H, HD); lfs = st("lfs")[None]
    return (yp, ys, csp, css, kp, vp, lfp, ks, vs, lfs)
```

```python
from contextlib import ExitStack
import numpy as np
import concourse.bass as bass
import concourse.mybir as mybir
from concourse.bass_utils import run_bass_kernel_spmd

F32 = mybir.dt.float32
BF16 = mybir.dt.bfloat16
AF = mybir.ActivationFunctionType
ALU = mybir.AluOpType
AX = mybir.AxisListType

D = 1024
KC = 8
FF = 2816
H = 16
HD = 64
PL = 2048
SL = 64
NT = 1088
EPS = 1e-6
NCORES = 8


class Trk:
    def __init__(self, nc, es):
        self.nc = nc
        self.engs = {"pe": nc.tensor, "act": nc.scalar, "dve": nc.vector, "pool": nc.gpsimd, "sp": nc.sync}
        self.sem = {k: es.enter_context(nc.semaphore("s_" + k)) for k in self.engs}
        self.cnt = {k: 0 for k in self.engs}
        self.known = {k: {} for k in self.engs}
        self.res = {}
        self.dsems = {q: [es.enter_context(nc.semaphore("d_%s%d" % (q, i))) for i in range(n)]
                      for q, n in (("sp", 14), ("pool", 14))}
        self.dval = {q: [0] * len(v) for q, v in self.dsems.items()}
        self.dnext = {q: 0 for q in self.dsems}
        self.outs = []

    def _waits(self, eng, reads, writes):
        waits = {}

        def need(ev, same_ok):
            if ev is None:
                return
            sem, val, e = ev
            if e == eng and same_ok:
                return
            k = id(sem)
            if self.known[eng].get(k, 0) >= val:
                return
            if k not in waits or waits[k][1] < val:
                waits[k] = (sem, val)

        for r in reads:
            st = self.res.get(r)
            if st:
                need(st[0], False)
                if isinstance(r, tuple) and r[0] == "ps":
                    for ev in st[1].values():
                        need(ev, True)
        for w in writes:
            st = self.res.get(w)
            if st:
                need(st[0], True)
                for ev in st[1].values():
                    need(ev, True)
        for k, (sem, val) in waits.items():
            self.engs[eng].wait_ge(sem, val)
            self.known[eng][k] = val

    def _record(self, ev, who, reads, writes):
        for r in reads:
            st = self.res.setdefault(r, [None, {}])
            st[1][who] = ev
        for w in writes:
            self.res[w] = [ev, {}]

    def op(self, eng, fn, reads=(), writes=()):
        self._waits(eng, reads, writes)
        ins = fn(self.engs[eng])
        self.cnt[eng] += 1
        ins.then_inc(self.sem[eng], 1)
        ev = (self.sem[eng], self.cnt[eng], eng)
        self._record(ev, eng, reads, writes)
        return ev

    def dma(self, q, out, in_, reads=(), writes=(), is_out=False, **kw):
        i = self.dnext[q]
        self.dnext[q] = (i + 1) % len(self.dsems[q])
        sem = self.dsems[q][i]
        prev = self.dval[q][i]
        if prev > 0 and self.known[q].get(id(sem), 0) < prev:
            self.engs[q].wait_ge(sem, prev)
            self.known[q][id(sem)] = prev
        self._waits(q, reads, writes)
        ins = self.engs[q].dma_start(out=out, in_=in_, **kw)
        ins.then_inc(sem, 16)
        self.dval[q][i] = prev + 16
        ev = (sem, prev + 16, "dma_%s_%d_%d" % (q, i, prev))
        self._record(ev, ev[2], reads, writes)
        if is_out:
            self.outs.append(ev)
        return ev

    def reset_for_write(self, eng, keys):
        self._waits(eng, [], keys)
        for k in keys:
            self.res[k] = [None, {}]

    def finish(self):
        for sem, val, _ in self.outs:
            if self.known["sp"].get(id(sem), 0) < val:
                self.engs["sp"].wait_ge(sem, val)
                self.known["sp"][id(sem)] = val


class Ring:
    def __init__(self, nslots, plan):
        self.n = nslots
        self.plan = plan
        self.nl = 0
        self.nu = 0

    def prefetch(self):
        while self.nl < len(self.plan) and self.nl < self.nu + self.n:
            self.plan[self.nl](self.nl % self.n)
            self.nl += 1

    def acquire(self, off=0):
        self.prefetch()
        assert self.nl > self.nu + off
        return (self.nu + off) % self.n

    def release(self):
        self.nu += 1
        self.prefetch()


def build_nc():
    nc = bass.Bass("TRN2", target_bir_lowering=False)
    dt = lambda n, s, k="ExternalInput": nc.dram_tensor(n, s, F32, kind=k).ap()
    xp_d = dt("xp", [PL, D]); xs_d = dt("xs", [SL, D]); sconv_d = dt("sconv", [2, D])
    ck_d = dt("ck", [PL, D]); cv_d = dt("cv", [PL, D]); clf_d = dt("clf", [PL, H])
    nffn_d = dt("norm_ffn", [2, 2, D]); wfi_d = dt("ffn_w_in", [2, 2, D, 2 * FF]); wfo_d = dt("ffn_w_out", [2, 2, FF, D])
    nmix_d = dt("norm_mix", [2, D]); cwi_d = dt("conv_w_in", [D, 3 * D]); cw_d = dt("conv_w", [3, D])
    cwo_d = dt("conv_w_out", [D, D]); awi_d = dt("attn_w_in", [D, 3 * D + H]); abf_d = dt("attn_b_f", [H])
    qn_d = dt("q_norm", [HD]); kn_d = dt("k_norm", [HD]); awo_d = dt("attn_w_out", [D, D])
    O = "ExternalOutput"
    yp_d = dt("yp", [PL, D], O); ys_d = dt("ys", [SL, D], O); csp_d = dt("csp", [2, D], O); css_d = dt("css", [2, D], O)
    kp_d = dt("kp", [PL, D], O); vp_d = dt("vp", [PL, D], O); lfp_d = dt("lfp", [PL, H], O)
    ks_d = dt("ks", [SL, D], O); vs_d = dt("vs", [SL, D], O); lfs_d = dt("lfs", [SL, H], O)

    with ExitStack() as es:
        sb = lambda n, s, d=F32: es.enter_context(nc.sbuf_tensor(n, s, d))
        x = sb("x", [128, 9, D])
        xnT = sb("xnT", [128, KC, NT], BF16)
        mT = sb("mT", [128, KC, NT], BF16)
        gT = sb("gT", [128, 12 * NT], BF16)
        Wo = sb("Wo", [128, 12, D], BF16)
        ringb = [sb("ring%d" % i, [128, 4096], BF16) for i in range(3)]
        gtile = [sb("gtile%d" % i, [128, D]) for i in range(2)]
        xns = [sb("xns%d" % i, [128, D], BF16) for i in range(3)]
        sq = sb("sq", [128, 512])
        stmp = [sb("stmp%d" % i, [128, 512]) for i in range(3)]
        ub = [sb("ub%d" % i, [128, 516]) for i in range(2)]
        tA = sb("tA", [128, 512])
        kTr = [sb("kTr%d" % i, [128, 17 * 128], BF16) for i in range(2)]
        qTr = [sb("qTr%d" % i, [128, NT], BF16) for i in range(2)]
        PT = [sb("PT%d" % i, [128, 512], BF16) for i in range(3)]
        Otok = sb("Otok", [128, 9, 256], BF16)
        kst = [sb("kst%d" % i, [128, 256]) for i in range(2)]
        vst = [sb("vst%d" % i, [128, 256]) for i in range(2)]
        small = [sb("small%d" % i, [128, 64]) for i in range(6)]
        lfst = [sb("lfst%d" % i, [128, 16]) for i in range(5)]
        c_p = sb("c_p", [128, 16, 16]); nb_p = sb("nb_p", [128, 16, 16])
        c_s = sb("c_s", [128, 17, 16]); nb_s = sb("nb_s", [128, 17, 16])
        cs_p = sb("cs_p", [128, 16, 16, 3], BF16); cs_s = sb("cs_s", [128, 1, 16, 3], BF16)
        lfc = sb("lfc", [128, 16, 16])
        R_p = sb("R_p", [128, 16]); R_s = sb("R_s", [128, 16])
        ident = sb("ident", [128, 128], BF16); tri = sb("tri", [128, 128], BF16)
        identf = sb("identf", [128, 128]); U = sb("U", [128, 128]); ones = sb("ones", [128, 128])
        epsT = sb("epsT", [128, 1]); bfT = sb("bfT", [128, 16]); gq8 = sb("gq8", [128, 64]); gkT = sb("gkT", [128, 64])
        negM16 = sb("negM16", [128, 16]); mtmp = sb("mtmp", [128, 8])
        histp = sb("histp", [128, 2, 8]); hsin = sb("hsin", [128, 2, 8]); hsout = sb("hsout", [128, 2, 8])
        wk = sb("wk", [128, 3, 8]); oneT = sb("oneT", [128, 1])
        ps = [es.enter_context(nc.psum_tensor("ps%d" % i, [128, 512], F32)) for i in range(8)]
        T = Trk(nc, es)

        KG0 = 0; VG0 = 17 * 256; QG0 = VG0 + 17 * 4 * 66
        Kg = lambda: gT[:, KG0:KG0 + 17 * 256].rearrange("p (t c) -> p t c", c=256)
        Vg = lambda: gT[:, VG0:VG0 + 17 * 264].rearrange("p (t h c) -> p t h c", h=4, c=66)
        Qg = lambda: gT[:, QG0:QG0 + 9 * 272].rearrange("p (t h c) -> p t h c", h=4, c=68)
        gTv = lambda: gT[:, :].rearrange("p (f n) -> p f n", n=NT)

        st = {"ta": 0, "pb": 0, "ev": 0, "ss": 0, "stmp": 0, "xns": 0, "ub": 0, "pt": 0, "ob": 0, "kst": 0, "lf": 0, "ktr": 0, "qtr": 0}

        def rr(name, n):
            v = st[name]
            st[name] = (v + 1) % n
            return v

        st["npb"] = 8

        def pbank():
            v = st["pb"] % st["npb"]
            st["pb"] = (v + 1) % st["npb"]
            return v
        eveng = lambda: ("act", "dve")[rr("ev", 2)]

        def copy_op(eng, out, in_, reads, writes):
            if eng == "act":
                return T.op("act", lambda e: e.copy(out=out, in_=in_), reads, writes)
            return T.op(eng, lambda e: e.tensor_copy(out=out, in_=in_), reads, writes)

        def mm_group(out_ap, pairs, reads, writes):
            def fn(pe):
                n = len(pairs)
                for i, (l, r) in enumerate(pairs):
                    ins = pe.matmul(out_ap, l, r, start=(i == 0), stop=(i == n - 1))
                return ins
            return T.op("pe", fn, reads, writes)

        prologue_q = []
        T.op("pool", lambda e: e.memset(identf[:], 0.0), [], ["identf"])
        T.op("pool", lambda e: e.affine_select(out=identf[:], in_=identf[:], pattern=[[-1, 128]], compare_op=ALU.not_equal,
                                               fill=1.0, base=0, channel_multiplier=1), ["identf"], ["identf"])
        T.op("pool", lambda e: e.memset(epsT[:], EPS), [], ["eps"])
        T.op("pool", lambda e: e.tensor_copy(out=ident[:], in_=identf[:]), ["identf"], ["ident"])

        def consts_late():
            T.op("pool", lambda e: e.memset(U[:], 1.0), [], ["U"])
            T.op("pool", lambda e: e.affine_select(out=U[:], in_=U[:], pattern=[[1, 128]], compare_op=ALU.is_ge,
                                                   fill=0.0, base=0, channel_multiplier=-1), ["U"], ["U"])
            T.op("pool", lambda e: e.memset(ones[:], 1.0), [], ["ones"])
            T.op("pool", lambda e: e.memset(oneT[:], 1.0), [], ["oneT"])
            T.op("pool", lambda e: e.tensor_copy(out=tri[:], in_=U[:]), ["U"], ["tri"])
            T.op("pool", lambda e: e.memset(R_p[:], 0.0), [], ["R_p"])
            T.op("pool", lambda e: e.memset(R_s[:], 0.0), [], ["R_s"])
            T.op("pool", lambda e: e.memset(histp[:], 0.0), [], ["histp"])
            for i in range(2):
                T.op("pool", lambda e, i=i: e.memset(kTr[i][:], 1.0), [], [("kTr", i, j) for j in range(5)])
            T.dma("sp", bfT[:], abf_d.partition_broadcast(128), [], ["bfT"])
            T.dma("sp", gq8[:], qn_d.partition_broadcast(128), [], ["gq8"])
            T.dma("sp", gkT[:], kn_d.partition_broadcast(128), [], ["gkT"])
            for j in range(3):
                T.dma("sp", wk[:, j, :], cw_d[j].rearrange("(dc p) -> p dc", p=128), [], [("wk", j)], allow_slow_non_contiguous=True)
            for t in range(2):
                T.dma("sp", hsin[:, t, :], sconv_d[t].rearrange("(dc p) -> p dc", p=128), [], [("hsin", t)], allow_slow_non_contiguous=True)
            T.dma("sp", lfc[:], clf_d.rearrange("(t p) h -> p t h", p=128), [], ["lfc"])
            def m_compute():
                sa_ = rr("ss", 6); sb_ = rr("ss", 6)
                ka_ = ("sm", sa_); kb_ = ("sm", sb_)
                T.op("dve", lambda e: e.tensor_tensor(out=small[sa_][:, 0:64], in0=gq8[:], in1=gq8[:], op=ALU.mult), ["gq8"], [ka_])
                T.op("dve", lambda e: e.reduce_max(out=mtmp[:, 0:1], in_=small[sa_][:, 0:64], axis=AX.X), [ka_], ["mtmp"])
                T.op("dve", lambda e: e.tensor_tensor(out=small[sb_][:, 0:64], in0=gkT[:], in1=gkT[:], op=ALU.mult), ["gkT"], [kb_])
                T.op("dve", lambda e: e.reduce_max(out=mtmp[:, 1:2], in_=small[sb_][:, 0:64], axis=AX.X), [kb_], ["mtmp"])
                T.op("dve", lambda e: e.tensor_tensor(out=mtmp[:, 2:3], in0=mtmp[:, 0:1], in1=mtmp[:, 1:2], op=ALU.mult), ["mtmp"], ["mtmp"])
                T.op("act", lambda e: e.activation(out=mtmp[:, 3:4], in_=mtmp[:, 2:3], func=AF.Ln, scale=64.0), ["mtmp"], ["mtmp2"])
                T.op("act", lambda e: e.activation(out=mtmp[:, 4:5], in_=mtmp[:, 3:4], func=AF.Exp, scale=0.5), ["mtmp2"], ["mtmp3"])
                T.op("dve", lambda e: e.tensor_scalar(out=negM16[:], in0=ones[:, 0:16], scalar1=mtmp[:, 4:5], scalar2=None,
                                                      op0=ALU.mult), ["mtmp3", "ones"], ["negM16"])
                T.op("dve", lambda e: e.tensor_scalar(out=negM16[:], in0=negM16[:], scalar1=-1.0, scalar2=None,
                                                      op0=ALU.mult), ["negM16"], ["negM16"])
                T.op("dve", lambda e: e.tensor_scalar(out=gq8[:], in0=gq8[:], scalar1=0.125, scalar2=None, op0=ALU.mult), ["gq8", ka_], ["gq8"])


            prologue_q.append(m_compute)

        def wview(w2d, c0, n):
            return w2d[:, c0:c0 + n].rearrange("(kc p) n -> p kc n", p=128)

        ring_plan = []
        wo_plan = []
        RK = lambda slot: [("ring", slot, 0), ("ring", slot, 1), ("ring", slot, 2)]
        WK = [("wo", 0), ("wo", 1)]

        def ld_ffn_in(i, j, t):
            def f(slot):
                T.reset_for_write("pool", RK(slot))
                v = ringb[slot][:, :].rearrange("p (kc a n) -> p kc a n", kc=8, a=2)
                T.dma("pool", v[:, :, 0, :], wview(wfi_d[i, j], 256 * t, 256), [], [("ring", slot, 0)])
                T.dma("pool", v[:, :, 1, :], wview(wfi_d[i, j], FF + 256 * t, 256), [], [("ring", slot, 1)])
            return f

        def ld_conv_in(dc):
            def f(slot):
                T.reset_for_write("pool", RK(slot))
                v = ringb[slot][:, 0:3072].rearrange("p (kc a n) -> p kc a n", kc=8, a=3)
                for a in range(3):
                    T.dma("pool", v[:, :, a, :], wview(cwi_d, a * D + dc * 128, 128), [], [("ring", slot, a)])
            return f

        def ld_attn_a(g):
            def f(slot):
                T.reset_for_write("pool", RK(slot))
                v = ringb[slot][:, :].rearrange("p (kc a n) -> p kc a n", kc=8, a=2)
                T.dma("pool", v[:, :, 0, :], wview(awi_d, g * 256, 256), [], [("ring", slot, 0)])
                T.dma("pool", v[:, :, 1, :], wview(awi_d, D + g * 256, 256), [], [("ring", slot, 1)])
            return f

        def ld_attn_b(g):
            def f(slot):
                T.reset_for_write("pool", RK(slot))
                v = ringb[slot][:, 0:8 * 272].rearrange("p (kc n) -> p kc n", kc=8)
                T.dma("pool", v[:, :, 0:256], wview(awi_d, 2 * D + g * 256, 256), [], [("ring", slot, 0)])
                if g == 0:
                    T.dma("pool", v[:, :, 256:272], wview(awi_d, 3 * D, 16), [], [("ring", slot, 1)])
            return f

        def ld_wo(src2d, r0, nfc):
            def f(slot):
                T.reset_for_write("pool", WK)
                h1 = nfc // 2
                for pi, (a, b_) in enumerate(((0, h1), (h1, nfc))):
                    T.dma("pool", Wo[:, a:b_, :], src2d[r0 + a * 128:r0 + b_ * 128, :].rearrange("(f p) n -> p f n", p=128),
                          [], [("wo", pi)])
            return f

        FPARTS = ((0, 12), (12, 10))
        for h in range(2):
            for i in range(2):
                for j in range(2):
                    for (f0, nf) in FPARTS:
                        for t in range(f0 // 2, (f0 + nf) // 2):
                            ring_plan.append(ld_ffn_in(i, j, t))
                        wo_plan.append(ld_wo(wfo_d[i, j], f0 * 128, nf))
                    if j == 0:
                        if i == 0:
                            for dc in range(8):
                                ring_plan.append(ld_conv_in(dc))
                            wo_plan.append(ld_wo(cwo_d, 0, 8))
                        else:
                            for g in range(4):
                                ring_plan.append(ld_attn_a(g))
                                ring_plan.append(ld_attn_b(g))
                            wo_plan.append(ld_wo(awo_d, 0, 8))
        ring = Ring(3, ring_plan)
        wor = Ring(1, wo_plan)

        def tiles(h):
            return [(lt, 128) for lt in range(8)] + ([(8, 64)] if h == 1 else [])

        def groups(h):
            g = [(0, 512, [0, 1, 2, 3]), (512, 512, [4, 5, 6, 7])]
            if h == 1:
                g.append((1024, 64, [8]))
            return g

        tg_of = lambda lt: 2 if lt == 8 else lt // 4

        pending = []
        norm_q = []
        reload_q = []

        def do_reload(a):
            lt, rows, gk_ = a
            T.dma("sp", x[:, lt, :], xp_d[(8 + lt) * 128:(8 + lt + 1) * 128, :], [], [("x", lt)])
            norm_q.append(a)
            while len(norm_q) > 2:
                b_ = norm_q.pop(0)
                flush_pending(keep=2)
                norm_tile(*b_)

        def need_xnT(lt):
            while any(p[0] == lt for p in pending):
                p = pending.pop(0)
                p[1]()

        def flush_pending(keep=0):
            while len(pending) > keep:
                pending.pop(0)[1]()

        st["g"] = 0

        def start_norm(vec_ap):
            k = rr("g", 2)
            T.dma("sp", gtile[k][:], vec_ap.partition_broadcast(128), [], [("gtile", k)])
            return k

        def norm_tile(lt, rows, gk):
            si = rr("ss", 6)
            sm = small[si]
            T.op("act", lambda e: e.activation(out=sq[:rows, :].bitcast(BF16), in_=x[:rows, lt, :], func=AF.Square,
                                               accum_out=sm[:rows, 0:1]), [("x", lt)], ["sq", ("sm", si)])
            T.op("act", lambda e: e.activation(out=sm[:rows, 1:2], in_=sm[:rows, 0:1], func=AF.Ln, scale=1.0 / D,
                                               bias=epsT[:rows, 0:1]), [("sm", si), "eps"], [("sm", si)])
            T.op("act", lambda e: e.activation(out=sm[:rows, 2:3], in_=sm[:rows, 1:2], func=AF.Exp, scale=-0.5),
                 [("sm", si)], [("sm", si)])
            xi = rr("xns", 7)
            if xi < 3:
                xb = xns[xi]; xks = [("xns", xi)]
            elif xi < 5:
                xb = qTr[xi - 3][:, 0:1024]; xks = [("qTr", xi - 3, j) for j in range(3)]
            else:
                o4 = (xi - 5) * 4
                xb = Otok[:, o4:o4 + 4, :].rearrange("p t c -> p (t c)"); xks = [("Otok", o4 + j) for j in range(4)]
            T.op("dve", lambda e: e.scalar_tensor_tensor(out=xb[:rows, :], in0=x[:rows, lt, :], scalar=sm[:rows, 2:3],
                                                         in1=gtile[gk][:rows, :], op0=ALU.mult, op1=ALU.mult),
                 [("x", lt), ("sm", si), ("gtile", gk)], xks)

            def stage_b():
                b = pbank()
                psb = ps[b][:, :].bitcast(BF16)

                def fn(pe):
                    for kc in range(8):
                        ins = pe.transpose(psb[:, kc * 128:kc * 128 + rows], xb[:rows, kc * 128:(kc + 1) * 128],
                                           ident[:rows, :rows])
                    return ins
                T.op("pe", fn, xks + ["ident"], [("ps", b)])
                copy_op(eveng(), xnT[:, :, lt * 128:lt * 128 + rows],
                        psb.rearrange("p (k n) -> p k n", n=128)[:, :, 0:rows], [("ps", b)], [("xnT", lt)])
            pending.append((lt, stage_b))

        def after_tile(post, lt, rows):
            if post is None:
                return
            if post[0] == "norm":
                flush_pending(keep=2)
                norm_tile(lt, rows, post[1])
            else:
                h = post[1]
                if lt < 8:
                    T.dma("sp", yp_d[(h * 8 + lt) * 128:(h * 8 + lt + 1) * 128, :], x[:, lt, :], [("x", lt)], [], is_out=True)
                    if h == 0:
                        reload_q.append((lt, rows, post[2]))
                        while len(reload_q) > 1:
                            do_reload(reload_q.pop(0))
                else:
                    T.dma("sp", ys_d[:, :], x[:64, lt, :], [("x", lt)], [], is_out=True)

        def ffn(h, post):
            gv = gTv()
            for pi, (f0, nf) in enumerate(FPARTS):
                for t in range(nf // 2):
                    slot = ring.acquire()
                    wv = ringb[slot][:, :].rearrange("p (kc a n) -> p kc a n", kc=8, a=2)
                    for (c0, n, tl) in groups(h):
                        for lt in tl:
                            need_xnT(lt)
                        xk = [("xnT", lt) for lt in tl]
                        for sub in range(2):
                            fcl = 2 * t + sub
                            bg = pbank(); bu = pbank()
                            mm_group(ps[bg][:, 0:n], [(wv[:, kc, 0, sub * 128:(sub + 1) * 128], xnT[:, kc, c0:c0 + n]) for kc in range(8)],
                                     RK(slot) + xk, [("ps", bg)])
                            mm_group(ps[bu][:, 0:n], [(wv[:, kc, 1, sub * 128:(sub + 1) * 128], xnT[:, kc, c0:c0 + n]) for kc in range(8)],
                                     RK(slot) + xk, [("ps", bu)])
                            si = rr("stmp", 3)
                            T.op("act", lambda e: e.activation(out=stmp[si][:, 0:n], in_=ps[bg][:, 0:n], func=AF.Silu),
                                 [("ps", bg)], [("stmp", si)])
                            T.op("dve", lambda e: e.tensor_tensor(out=gv[:, fcl, c0:c0 + n], in0=stmp[si][:, 0:n], in1=ps[bu][:, 0:n],
                                                                  op=ALU.mult), [("stmp", si), ("ps", bu)], [("gT", fcl, c0)])
                    ring.release()
                    if prologue_q:
                        prologue_q.pop(0)()
                wor.acquire()
                for (lt, rows) in tiles(h):
                    c0g = groups(h)[tg_of(lt)][0]
                    for ch in range(2):
                        b = pbank()
                        mm_group(ps[b][:rows, :], [(gv[:, fcl, lt * 128:lt * 128 + rows], Wo[:, fcl, ch * 512:(ch + 1) * 512]) for fcl in range(nf)],
                                 WK + [("gT", fcl, c0g) for fcl in range(nf)], [("ps", b)])
                        T.op("dve", lambda e: e.scalar_tensor_tensor(out=x[:rows, lt, ch * 512:(ch + 1) * 512], in0=ps[b][:rows, :], scalar=0.5,
                                                                     in1=x[:rows, lt, ch * 512:(ch + 1) * 512], op0=ALU.mult, op1=ALU.add),
                             [("ps", b), ("x", lt)], [("x", lt)])
                    if pi == 1:
                        after_tile(post, lt, rows)
                wor.release()

        def out_proj(h, post):
            wor.acquire()
            for (lt, rows) in tiles(h):
                for ch in range(2):
                    b = pbank()
                    mm_group(ps[b][:rows, :], [(mT[:, dc, lt * 128:lt * 128 + rows], Wo[:, dc, ch * 512:(ch + 1) * 512]) for dc in range(8)],
                             WK + [("mT", lt)], [("ps", b)])
                    T.op("dve", lambda e: e.tensor_tensor(out=x[:rows, lt, ch * 512:(ch + 1) * 512], in0=ps[b][:rows, :],
                                                          in1=x[:rows, lt, ch * 512:(ch + 1) * 512], op=ALU.add),
                         [("ps", b), ("x", lt)], [("x", lt)])
                after_tile(post, lt, rows)
            wor.release()

        WKK = [("wk", 0), ("wk", 1), ("wk", 2)]

        def conv(h):
            for dc in range(8):
                slot = ring.acquire()
                wv = ringb[slot][:, 0:3072].rearrange("p (kc a n) -> p kc a n", kc=8, a=3)
                prev_ub = None
                gl = list(enumerate(groups(h)))
                if h == 1:
                    gl = [gl[2], gl[0], gl[1]]
                for gi, (c0, n, tl) in gl:
                    for lt in tl:
                        need_xnT(lt)
                    xk = [("xnT", lt) for lt in tl]
                    bb = [pbank() for _ in range(3)]
                    for a in range(3):
                        mm_group(ps[bb[a]][:, 0:n], [(wv[:, kc, a, :], xnT[:, kc, c0:c0 + n]) for kc in range(8)],
                                 RK(slot) + xk, [("ps", bb[a])])
                    ui = rr("ub", 2)
                    u = ub[ui]
                    if gi == 2:
                        copy_op("act", u[:, 0:2], hsin[:, :, dc], [("hsin", 0), ("hsin", 1)], [("ub", ui)])
                    elif gi == 0:
                        copy_op("act", u[:, 0:2], histp[:, :, dc], ["histp"], [("ub", ui)])
                    else:
                        copy_op("act", u[:, 0:2], prev_ub[0][:, prev_ub[1]:prev_ub[1] + 2], [("ub", prev_ub[2])], [("ub", ui)])
                    si = rr("stmp", 3)
                    copy_op("act", stmp[si][:, 0:n], ps[bb[1]][:, 0:n], [("ps", bb[1])], [("stmp", si)])
                    T.op("dve", lambda e: e.tensor_tensor(out=u[:, 2:2 + n], in0=stmp[si][:, 0:n], in1=ps[bb[2]][:, 0:n], op=ALU.mult),
                         [("stmp", si), ("ps", bb[2])], [("ub", ui)])
                    tAb, tAk = ((tA, "tA"), (sq, "sq"))[rr("ta", 2)]
                    T.op("act", lambda e: e.activation(out=tAb[:, 0:n], in_=u[:, 0:n], func=AF.Copy, scale=wk[:, 0, dc:dc + 1]),
                         [("ub", ui)] + WKK, [tAk])
                    T.op("dve", lambda e: e.scalar_tensor_tensor(out=tAb[:, 0:n], in0=u[:, 1:1 + n], scalar=wk[:, 1, dc:dc + 1], in1=tAb[:, 0:n],
                                                                 op0=ALU.mult, op1=ALU.add), [("ub", ui), tAk] + WKK, [tAk])
                    T.op("dve", lambda e: e.scalar_tensor_tensor(out=tAb[:, 0:n], in0=u[:, 2:2 + n], scalar=wk[:, 2, dc:dc + 1], in1=tAb[:, 0:n],
                                                                 op0=ALU.mult, op1=ALU.add), [("ub", ui), tAk] + WKK, [tAk])
                    T.op("dve", lambda e: e.tensor_tensor(out=mT[:, dc, c0:c0 + n], in0=tAb[:, 0:n], in1=ps[bb[0]][:, 0:n], op=ALU.mult),
                         [tAk, ("ps", bb[0])], [("mT", lt) for lt in tl])
                    if gi == 1:
                        copy_op("act", histp[:, :, dc], u[:, n:n + 2], [("ub", ui)], ["histp"])
                    if gi == 2:
                        copy_op("act", hsout[:, :, dc], u[:, n:n + 2], [("ub", ui)], ["hsout"])
                    if gi != 2:
                        prev_ub = (u, n, ui)
                ring.release()

        def cumsum_tile(lf_ap, rows, Rt, Rkey, c_ap, ckey, lfkeys):
            b = pbank()

            def fn(pe):
                pe.matmul(ps[b][:rows, 0:16], U[:rows, :rows], lf_ap, start=True, stop=False)
                return pe.matmul(ps[b][:rows, 0:16], ones[:, :rows], Rt[:, :], start=False, stop=True)
            T.op("pe", fn, lfkeys + [Rkey, "U", "ones"], [("ps", b)])
            T.op("dve", lambda e: e.tensor_copy(out=c_ap, in_=ps[b][:rows, 0:16]), [("ps", b)], [ckey])
            T.op("dve", lambda e: e.tensor_tensor(out=Rt[:rows, :], in0=Rt[:rows, :], in1=lf_ap, op=ALU.add), lfkeys + [Rkey], [Rkey])

        def cumsum_post(rows, nt_, c_v, nb_v, ckeys, cs_v=None):
            T.op("dve", lambda e: e.scalar_tensor_tensor(out=nb_v, in0=c_v, scalar=-1.0,
                                                         in1=negM16[:rows, :].unsqueeze(1).to_broadcast([rows, nt_, 16]), op0=ALU.mult, op1=ALU.add),
                 ckeys + ["negM16"], [(k, "nb") for k in ckeys])
            if cs_v is not None:
                si = rr("stmp", 3)
                n_ = nt_ * 16
                r1 = stmp[si][:rows, 0:n_].rearrange("p (t h) -> p t h", h=16)
                r2 = stmp[si][:rows, 128:128 + n_].rearrange("p (t h) -> p t h", h=16)
                k = ("stmp", si)
                T.op("dve", lambda e: e.tensor_copy(out=cs_v[:, :, :, 0], in_=c_v), ckeys, [(kk, "cs") for kk in ckeys])
                T.op("dve", lambda e: e.tensor_tensor(out=r1, in0=c_v, in1=cs_v[:, :, :, 0], op=ALU.subtract), ckeys + [(kk, "cs") for kk in ckeys], [k])
                T.op("dve", lambda e: e.tensor_copy(out=cs_v[:, :, :, 1], in_=r1), [k], [(kk, "cs1") for kk in ckeys])
                T.op("dve", lambda e: e.tensor_tensor(out=r2, in0=r1, in1=cs_v[:, :, :, 1], op=ALU.subtract), [k] + [(kk, "cs1") for kk in ckeys], [(k, 2)])
                T.op("dve", lambda e: e.tensor_copy(out=cs_v[:, :, :, 2], in_=r2), [(k, 2), k], [(kk, "cs2") for kk in ckeys])

        def attn_job(g, q_tiles, key_tiles, nb, cname, batch=False):
            Kv = Kg(); Vv = Vg(); Qv = Qg()
            nk = len(key_tiles)
            LA = 3
            slots = {}
            PTs = PT + [tA[:, :].bitcast(BF16)]
            PTk = [("PT", 0), ("PT", 1), ("PT", 2), "tA"]

            def emit_transposes(hh):
                ks = rr("ktr", 2); qs = rr("qtr", 2)
                slots[hh] = (ks, qs)
                for b0 in range(0, nk, 4):
                    chunk = key_tiles[b0:b0 + 4]
                    b = pbank()
                    psb = ps[b][:, :].bitcast(BF16)

                    def fn(pe, chunk=chunk, psb=psb):
                        for j, (idx, rows, _, _) in enumerate(chunk):
                            ins = pe.transpose(psb[0:64, j * 128:j * 128 + rows], Kv[:rows, idx, hh * 64:(hh + 1) * 64], ident[:rows, :rows])
                        return ins
                    T.op("pe", fn, [("Kg", idx) for (idx, _, _, _) in chunk] + ["ident"], [("ps", b)])
                    ncol = (len(chunk) - 1) * 128 + chunk[-1][1]
                    copy_op("dve", kTr[ks][0:64, b0 * 128:b0 * 128 + ncol], psb[0:64, 0:ncol], [("ps", b)], [("kTr", ks, b0 // 4)])
                for b0 in range(0, len(q_tiles), 4):
                    chunk = q_tiles[b0:b0 + 4]
                    b = pbank()
                    psb = ps[b][:, :].bitcast(BF16)

                    def fn(pe, chunk=chunk, psb=psb):
                        for j, (lt, rows) in enumerate(chunk):
                            ins = pe.transpose(psb[0:67, j * 128:j * 128 + rows], Qv[:rows, lt, hh, 0:67], ident[:rows, :rows])
                        return ins
                    T.op("pe", fn, [("Qg", lt) for (lt, _) in chunk] + ["ident"], [("ps", b)])
                    ncol = (len(chunk) - 1) * 128 + chunk[-1][1]
                    copy_op("dve", qTr[qs][0:67, b0 * 128:b0 * 128 + ncol], psb[0:67, 0:ncol], [("ps", b)], [("qTr", qs, b0 // 4)])

            steps = []
            for hh in range(4):
                for q0 in range(0, len(q_tiles), 4):
                    qt = q_tiles[q0:q0 + 4]
                    last_pos = q0 + len(qt) - 1
                    rel = [kt for kt in key_tiles if kt[3] is None or kt[3] <= last_pos]
                    grp = {"ob": None}
                    if batch:
                        pk_ = [kt for kt in rel if kt[3] is None]
                        ow_ = [kt for kt in rel if kt[3] is not None]
                        ki_ = 0
                        for b0 in range(0, len(pk_), 8):
                            steps.append({"hh": hh, "q0": q0, "qt": qt, "ki": ki_, "kt": pk_[b0], "batch": pk_[b0:b0 + 8], "last": False,
                                          "grp": grp, "newhead": (q0 == 0 and ki_ == 0)})
                            ki_ += 1
                        for kt in ow_:
                            steps.append({"hh": hh, "q0": q0, "qt": qt, "ki": ki_, "kt": kt, "batch": None, "last": kt is ow_[-1], "grp": grp,
                                          "newhead": False})
                            ki_ += 1
                        continue
                    for ki_, kt in enumerate(rel):
                        steps.append({"hh": hh, "q0": q0, "qt": qt, "ki": ki_, "kt": kt, "batch": None, "last": ki_ == len(rel) - 1, "grp": grp,
                                      "newhead": (q0 == 0 and ki_ == 0)})

            def front(s):
                hh = s["hh"]; q0 = s["q0"]; qt = s["qt"]
                ks, qs = slots[hh]
                head = g * 4 + hh
                if s["batch"]:
                    nq = sum(r for _, r in qt)
                    b = pbank()
                    bl = s["batch"]

                    def fnq(pe):
                        for j, kt in enumerate(bl):
                            kcol = key_tiles.index(kt) * 128
                            ins = pe.matmul(ps[b][:, j * nq:(j + 1) * nq], kTr[ks][0:67, kcol:kcol + 128],
                                            qTr[qs][0:67, q0 * 128:q0 * 128 + nq], start=True, stop=True)
                        return ins
                    T.op("pe", fnq, [("kTr", ks, key_tiles.index(kt) // 4) for kt in bl] + [("qTr", qs, q0 // 4)], [("ps", b)])
                    pi = rr("pt", 4)
                    T.op("act", lambda e: e.activation(out=PTs[pi][:, 0:len(bl) * nq], in_=ps[b][:, 0:len(bl) * nq], func=AF.Exp,
                                                       bias=negM16[:, 0:1], scale=1.0), [("ps", b), "negM16"], [PTk[pi]])
                    s["pi"] = pi; s["s_t"] = 0
                    return
                idx, krows, nbidx, own_pos = s["kt"]
                nq = sum(r for _, r in qt)
                kcol = key_tiles.index(s["kt"]) * 128
                s_t = 0 if (own_pos is None or own_pos < q0) else own_pos - q0
                start_c = s_t * 128
                N = nq - start_c
                b = pbank()
                T.op("pe", lambda pe: pe.matmul(ps[b][:krows, 0:N], kTr[ks][0:67, kcol:kcol + krows],
                                                qTr[qs][0:67, q0 * 128 + start_c:q0 * 128 + nq], start=True, stop=True),
                     [("kTr", ks, kcol // 512), ("qTr", qs, q0 // 4)], [("ps", b)])
                pi = rr("pt", 4)
                T.op("act", lambda e: e.activation(out=PTs[pi][:krows, 0:N], in_=ps[b][:krows, 0:N], func=AF.Exp,
                                                   bias=nb[:krows, nbidx, head:head + 1], scale=1.0),
                     [("ps", b), ((cname, nbidx), "nb")], [PTk[pi]])
                if own_pos is not None and own_pos >= q0:
                    dq = qt[s_t][1]
                    T.op("dve", lambda e: e.tensor_tensor(out=PTs[pi][:krows, 0:dq], in0=PTs[pi][:krows, 0:dq], in1=tri[:krows, 0:dq],
                                                          op=ALU.mult), [PTk[pi], "tri"], [PTk[pi]])
                s["pi"] = pi; s["s_t"] = s_t

            def back(s):
                hh = s["hh"]; q0 = s["q0"]; qt = s["qt"]; ki_ = s["ki"]; pi = s["pi"]; s_t = s["s_t"]
                idx, krows, nbidx, own_pos = s["kt"]
                if s["grp"]["ob"] is None:
                    s["grp"]["ob"] = 6 + rr("ob", 2)
                ob = s["grp"]["ob"]

                if s["batch"]:
                    bl = s["batch"]
                    nq = sum(r for _, r in qt)

                    def fnb(pe):
                        for j, kt in enumerate(bl):
                            ins = pe.matmul(ps[ob][:nq, 0:65], PTs[pi][:, j * nq:(j + 1) * nq], Vv[:, kt[0], hh, 0:65],
                                            start=(ki_ == 0 and j == 0), stop=False, skip_group_check=True)
                        return ins
                    T.op("pe", fnb, [PTk[pi]] + [("Vg", kt[0], hh) for kt in bl], [("ps", ob)])
                    return

                def fn(pe):
                    for i in range(s_t, len(qt)):
                        qrows = qt[i][1]
                        first = (ki_ == 0 and i == 0)
                        lastk = (own_pos is not None and own_pos == q0 + i)
                        ins = pe.matmul(ps[ob][:qrows, i * 65:i * 65 + 65], PTs[pi][:krows, (i - s_t) * 128:(i - s_t) * 128 + qrows],
                                        Vv[:krows, idx, hh, 0:65], start=first, stop=lastk, skip_group_check=True)
                    return ins
                T.op("pe", fn, [PTk[pi], ("Vg", idx, hh)], [("ps", ob)])
                if s["last"]:
                    nt_ = len(qt)
                    qrows = qt[0][1]
                    lt0 = qt[0][0]
                    si = rr("ss", 6)
                    ov = ps[ob][:qrows, 0:nt_ * 65].rearrange("p (t c) -> p t c", c=65)
                    T.op("dve", lambda e: e.reciprocal(out=small[si][:qrows, 0:nt_], in_=ov[:, :, 64]), [("ps", ob)], [("sm", si)])
                    T.op("dve", lambda e: e.tensor_tensor(out=Otok[:qrows, lt0:lt0 + nt_, hh * 64:(hh + 1) * 64], in0=ov[:, :, 0:64],
                                                          in1=small[si][:qrows, 0:nt_].unsqueeze(2).to_broadcast([qrows, nt_, 64]), op=ALU.mult),
                         [("ps", ob), ("sm", si)], [("Otok", lt) for (lt, _) in qt])

            emit_transposes(0)
            for i in range(len(steps) + LA):
                if i < len(steps):
                    s = steps[i]
                    front(s)
                    if s["newhead"] and s["hh"] + 1 < 4:
                        emit_transposes(s["hh"] + 1)
                if i >= LA:
                    back(steps[i - LA])
            for (lt, rows) in q_tiles:
                b = pbank()
                psb = ps[b][:, :].bitcast(BF16)

                def fn(pe):
                    for j in range(2):
                        ins = pe.transpose(psb[:, j * 128:j * 128 + rows], Otok[:rows, lt, j * 128:(j + 1) * 128], ident[:rows, :rows])
                    return ins
                T.op("pe", fn, [("Otok", lt), "ident"], [("ps", b)])
                copy_op(eveng(), mT[:, 2 * g:2 * g + 2, lt * 128:lt * 128 + rows],
                        psb[:, 0:256].rearrange("p (k n) -> p k n", n=128)[:, :, 0:rows], [("ps", b)], [("mT", lt)])

        def attention(h):
            Kv = Kg(); Vv = Vg(); Qv = Qg()
            npast = 8 * h
            st["npb"] = 6
            cast_eng = "dve" if h == 1 else "act"
            T.op("dve", lambda e: e.memset(Vv[:, :, :, 64:65], 1.0), [],
                 ["gTfree"] + [("Vg", i, hh) for i in range(17) for hh in range(4)] + [("Kg", i) for i in range(17)] + [("Qg", i) for i in range(9)]
                 + [("gT", f, c) for f in range(12) for c in (0, 512, 1024)])
            for g in range(4):
                if h == 1:
                    T.op("dve", lambda e: e.memset(Vv[:, 0:16, :, 64:65], 1.0), ["gTfree"], [("Vg", i, hh) for i in range(16) for hh in range(4)])
                    T.dma("pool", Kv[:, 0:8, :], kp_d[0:1024, g * 256:(g + 1) * 256].rearrange("(t p) c -> p t c", p=128),
                          ["gTfree"] + [("kp_out", g, i) for i in range(8)], [("Kg", i) for i in range(8)])
                    for hh in range(4):
                        T.dma("pool", Vv[:, 0:8, hh, 0:64], vp_d[0:1024, g * 256 + hh * 64:g * 256 + (hh + 1) * 64].rearrange("(t p) c -> p t c", p=128),
                              ["gTfree"] + [("vp_out", g, i) for i in range(8)], [("Vg", i, hh) for i in range(8)])
                sa = ring.acquire(0)
                wa = ringb[sa][:, :].rearrange("p (kc n) -> p kc n", kc=8)
                sb_ = ring.acquire(1)
                wb = ringb[sb_][:, 0:8 * 272].rearrange("p (kc n) -> p kc n", kc=8)
                nB = 272 if g == 0 else 256
                tails = []
                st2 = []
                for (lt, rows) in tiles(h):
                    need_xnT(lt)
                    is_s = (lt == 8)
                    kidx = 16 if is_s else npast + lt
                    bA = pbank(); bB = pbank()
                    mm_group(ps[bA][:rows, 0:512], [(xnT[:, kc, lt * 128:lt * 128 + rows], wa[:, kc, :]) for kc in range(8)],
                             RK(sa) + [("xnT", lt)], [("ps", bA)])
                    mm_group(ps[bB][:rows, 0:nB], [(xnT[:, kc, lt * 128:lt * 128 + rows], wb[:, kc, 0:nB]) for kc in range(8)],
                             RK(sb_) + [("xnT", lt)], [("ps", bB)])
                    T.op("act", lambda e: e.activation(out=sq[:rows, :], in_=ps[bA][:rows, :], func=AF.Square), [("ps", bA)], ["sq"])
                    si = rr("ss", 6)
                    sm = small[si]
                    T.op("dve", lambda e: e.reduce_sum(out=sm[:rows, 0:8], in_=sq[:rows, :].rearrange("p (h c) -> p h c", c=64), axis=AX.X),
                         ["sq"], [("sm", si)])
                    T.op("act", lambda e: e.activation(out=sm[:rows, 8:16], in_=sm[:rows, 0:8], func=AF.Ln, scale=1.0 / HD, bias=epsT[:rows, 0:1]),
                         [("sm", si), "eps"], [("sm", si)])
                    T.op("act", lambda e: e.activation(out=sm[:rows, 16:24], in_=sm[:rows, 8:16], func=AF.Exp, scale=-0.5), [("sm", si)], [("sm", si)])
                    def stage2(lt=lt, rows=rows, is_s=is_s, kidx=kidx, bA=bA, bB=bB, si=si, sm=sm):
                        t1 = rr("stmp", 3)
                        T.op("dve", lambda e: e.tensor_tensor(out=stmp[t1][:rows, 0:256].rearrange("p (h c) -> p h c", c=64),
                                                              in0=ps[bA][:rows, 0:256].rearrange("p (h c) -> p h c", c=64),
                                                              in1=sm[:rows, 16:20].unsqueeze(2).to_broadcast([rows, 4, 64]), op=ALU.mult),
                             [("ps", bA), ("sm", si)], [("stmp", t1)])
                        T.op("dve", lambda e: e.tensor_tensor(out=Qv[:rows, lt, :, 0:64], in0=stmp[t1][:rows, 0:256].rearrange("p (h c) -> p h c", c=64),
                                                              in1=gq8[:rows, :].unsqueeze(1).to_broadcast([rows, 4, 64]), op=ALU.mult),
                             [("stmp", t1), "gq8"], [("Qg", lt)])
                        t2 = rr("stmp", 3)
                        T.op("dve", lambda e: e.tensor_tensor(out=stmp[t2][:rows, 0:256].rearrange("p (h c) -> p h c", c=64),
                                                              in0=ps[bA][:rows, 256:512].rearrange("p (h c) -> p h c", c=64),
                                                              in1=sm[:rows, 20:24].unsqueeze(2).to_broadcast([rows, 4, 64]), op=ALU.mult),
                             [("ps", bA), ("sm", si)], [("stmp", t2)])
                        ki = rr("kst", 3)
                        kb, kk = (kst[ki], ("kst", ki)) if ki < 2 else (ub[0][:, 0:256], ("ub", 0))
                        vb, vk = (vst[ki], ("vst", ki)) if ki < 2 else (ub[1][:, 0:256], ("ub", 1))
                        T.op("dve", lambda e: e.tensor_tensor(out=kb[:rows, :].rearrange("p (h c) -> p h c", c=64),
                                                              in0=stmp[t2][:rows, 0:256].rearrange("p (h c) -> p h c", c=64),
                                                              in1=gkT[:rows, :].unsqueeze(1).to_broadcast([rows, 4, 64]), op=ALU.mult),
                             [("stmp", t2), "gkT"], [kk])
                        copy_op(cast_eng, Kv[:rows, kidx, :], kb[:rows, :], [kk], [("Kg", kidx)])
                        copy_op("act", vb[:rows, :], ps[bB][:rows, 0:256], [("ps", bB)], [vk])
                        copy_op(cast_eng, Vv[:rows, kidx, :, 0:64], ps[bB][:rows, 0:256].rearrange("p (h c) -> p h c", c=64),
                                [("ps", bB)], [("Vg", kidx, hh) for hh in range(4)])
                        if is_s:
                            T.dma("sp", ks_d[:, g * 256:(g + 1) * 256], kb[:rows, :], [kk], [], is_out=True)
                            T.dma("sp", vs_d[:, g * 256:(g + 1) * 256], vb[:rows, :], [vk], [], is_out=True)
                        else:
                            r0 = (h * 8 + lt) * 128
                            T.dma("sp", kp_d[r0:r0 + 128, g * 256:(g + 1) * 256], kb[:, :], [kk], [("kp_out", g, lt)] if h == 0 else [], is_out=True)
                            T.dma("sp", vp_d[r0:r0 + 128, g * 256:(g + 1) * 256], vb[:, :], [vk], [("vp_out", g, lt)] if h == 0 else [], is_out=True)
                        if g == 0:
                            li = rr("lf", 5)
                            lf = lfst[li]
                            s2 = rr("ss", 6)
                            T.op("dve", lambda e: e.tensor_tensor(out=small[s2][:rows, 0:16], in0=ps[bB][:rows, 256:272], in1=bfT[:rows, :], op=ALU.add),
                                 [("ps", bB), "bfT"], [("sm", s2)])
                            T.op("act", lambda e: e.activation(out=small[s2][:rows, 16:32], in_=small[s2][:rows, 0:16], func=AF.Exp, scale=-1.0),
                                 [("sm", s2)], [("sm", s2)])
                            T.op("act", lambda e: e.activation(out=small[s2][:rows, 32:48], in_=small[s2][:rows, 16:32], func=AF.Ln, bias=oneT[:rows, 0:1]),
                                 [("sm", s2), "oneT"], [("sm", s2)])
                            T.op("dve", lambda e: e.tensor_scalar(out=lf[:rows, :], in0=small[s2][:rows, 32:48], scalar1=-1.0, scalar2=None, op0=ALU.mult),
                                 [("sm", s2)], [("lf", li)])
                            if is_s:
                                T.dma("sp", lfs_d[:, :], lf[:rows, :], [("lf", li)], [], is_out=True)
                            else:
                                T.dma("sp", lfp_d[(h * 8 + lt) * 128:(h * 8 + lt + 1) * 128, :], lf[:, :], [("lf", li)], [], is_out=True)

                        def tail(lt=lt, rows=rows, is_s=is_s, li=(li if g == 0 else None)):
                            if g == 0:
                                if is_s:
                                    cumsum_tile(lfst[li][:rows, :], rows, R_s, "R_s", c_s[:rows, 16, :], ("c_s", 16), [("lf", li)])
                                else:
                                    gt_ = h * 8 + lt
                                    cumsum_tile(lfst[li][:rows, :], rows, R_p, "R_p", c_p[:rows, gt_, :], ("c_p", gt_), [("lf", li)])
                        tails.append(tail)
                        while len(tails) > 3:
                            tails.pop(0)()
                    st2.append(stage2)
                    while len(st2) > 1:
                        st2.pop(0)()
                while st2:
                    st2.pop(0)()
                while tails:
                    tails.pop(0)()
                pk = [("c_p", h * 8 + i) for i in range(8)]
                if g == 0:
                    cumsum_post(128, 8, c_p[:, h * 8:h * 8 + 8, :], nb_p[:, h * 8:h * 8 + 8, :], pk, cs_v=cs_p[:, h * 8:h * 8 + 8, :, :])
                    if h == 1:
                        cumsum_post(64, 1, c_s[:64, 16:17, :], nb_s[:64, 16:17, :], [("c_s", 16)], cs_v=cs_s[:64, 0:1, :, :])
                T.op("dve", lambda e: e.tensor_copy(out=Qv[:, 0:8, :, 64:67], in_=cs_p[:, h * 8:h * 8 + 8, g * 4:(g + 1) * 4, :]),
                     [(k, c) for k in pk for c in ("cs", "cs1", "cs2")], [("Qg", lt) for lt in range(8)])
                if h == 1:
                    T.op("dve", lambda e: e.tensor_copy(out=Qv[:64, 8, :, 64:67], in_=cs_s[:64, 0, g * 4:(g + 1) * 4, :]),
                         [(("c_s", 16), c) for c in ("cs", "cs1", "cs2")], [("Qg", 8)])
                ring.release()
                ring.release()
                ktl = [(i, 128, i, None) for i in range(npast)] + [(npast + i, 128, npast + i, i) for i in range(8)]
                attn_job(g, [(lt, 128) for lt in range(8)], ktl, nb_p, "c_p")
                if h == 1:
                    T.dma("pool", Kv[:, 0:16, :], ck_d[:, g * 256:(g + 1) * 256].rearrange("(t p) c -> p t c", p=128),
                          [], [("Kg", i) for i in range(16)])
                    for hh in range(4):
                        T.dma("pool", Vv[:, 0:16, hh, 0:64], cv_d[:, g * 256 + hh * 64:g * 256 + (hh + 1) * 64].rearrange("(t p) c -> p t c", p=128),
                              [], [("Vg", i, hh) for i in range(16)])
                    for hh in range(4):
                        T.op("dve", lambda e: e.tensor_tensor(out=Vv[:, 0:16, hh, 0:65], in0=Vv[:, 0:16, hh, 0:65],
                                                              in1=nb_s[:, 0:16, g * 4 + hh:g * 4 + hh + 1].to_broadcast([128, 16, 65]), op=ALU.mult),
                             ["w_s"] + [("Vg", i, hh) for i in range(16)], [("Vg", i, hh) for i in range(16)])
                    ktl = [(i, 128, i, None) for i in range(16)] + [(16, 64, 16, 0)]
                    attn_job(g, [(8, 64)], ktl, nb_s, "c_s", batch=True)

        for lt in range(4):
            T.dma("sp", x[:, lt, :], xp_d[lt * 128:(lt + 1) * 128, :], [], [("x", lt)])
        NV = [nffn_d[0, 0], nmix_d[0], nffn_d[0, 1], nffn_d[1, 0], nmix_d[1], nffn_d[1, 1]] * 2
        gks = {}

        def load_norm(i):
            if i < len(NV):
                gks[i] = start_norm(NV[i])
        load_norm(0)
        gk0 = gks[0]
        T._waits("sp", [("x", lt) for lt in range(4)], [])
        for lt in range(4, 8):
            T.dma("sp", x[:, lt, :], xp_d[lt * 128:(lt + 1) * 128, :], [], [("x", lt)])
        T.dma("sp", x[:64, 8, :], xs_d[:, :], [], [("x", 8)])
        load_norm(1)
        for n_ in (1, 2, 3):
            ring.n = n_
            ring.prefetch()
            if n_ < 3:
                T._waits("pool", RK(n_ - 1), [])
        wor.prefetch()
        consts_late()
        for t in range(16):
            prologue_q.append(lambda t=t: cumsum_tile(lfc[:, t, :], 128, R_s, "R_s", c_s[:, t, :], ("c_s", t), ["lfc"]))

        def prologue_tail():
            bce = pbank()
            T.op("pe", lambda pe: pe.matmul(ps[bce][:, 0:16], ones[:, :], R_s[:, :], start=True, stop=True), ["R_s", "ones"], [("ps", bce)])
            T.op("dve", lambda e: e.tensor_tensor(out=nb_s[:, 0:16, :], in0=c_s[:, 0:16, :], in1=ps[bce][:, 0:16].unsqueeze(1).to_broadcast([128, 16, 16]),
                                                  op=ALU.subtract), [("ps", bce)] + [("c_s", t) for t in range(16)], ["w_s"])
            T.op("act", lambda e: e.activation(out=nb_s[:, 0:16, :], in_=nb_s[:, 0:16, :], func=AF.Exp, scale=-1.0), ["w_s"], ["w_s"])
            T.op("dve", lambda e: e.memset(R_s[:], 0.0), [], ["R_s"])
        prologue_q.append(prologue_tail)

        for h in range(2):
            if h == 0:
                gk = gk0
                for (lt, rows) in tiles(h):
                    norm_tile(lt, rows, gk)
                    flush_pending(keep=2)
            else:
                norm_tile(8, 64, gk_next)
            base = 6 * h
            load_norm(base + 2)
            ffn(h, ("norm", gks[base + 1]))
            load_norm(base + 3)
            conv(h)
            out_proj(h, ("norm", gks[base + 2]))
            load_norm(base + 4)
            if h == 1:
                for t in range(2):
                    T.dma("sp", csp_d[t].rearrange("(dc p) -> p dc", p=128), histp[:, t, :], ["histp"], [], is_out=True,
                          allow_slow_non_contiguous=True)
                    T.dma("sp", css_d[t].rearrange("(dc p) -> p dc", p=128), hsout[:, t, :], ["hsout"], [], is_out=True,
                          allow_slow_non_contiguous=True)
            ffn(h, ("norm", gks[base + 3]))
            load_norm(base + 5)
            ffn(h, ("norm", gks[base + 4]))
            flush_pending()
            while prologue_q:
                prologue_q.pop(0)()
            load_norm(base + 6)
            attention(h)
            st["npb"] = 8
            out_proj(h, ("norm", gks[base + 5]))
            load_norm(base + 7)
            gk_next = gks.get(base + 6)
            ffn(h, ("out", h, gk_next))
            while reload_q:
                do_reload(reload_q.pop(0))
            while norm_q:
                a = norm_q.pop(0)
                flush_pending(keep=2)
                norm_tile(*a)
        T.finish()
    return nc


_NC = None


def kernel(x_prompt, x_sample, state_conv, cache_k, cache_v, cache_logf, norm_ffn, ffn_w_in, ffn_w_out, norm_mix,
           conv_w_in, conv_w, conv_w_out, attn_w_in, attn_b_f, q_norm, k_norm, attn_w_out):
    global _NC
    if _NC is None:
        _NC = build_nc()
    nc = _NC
    f = lambda a: np.ascontiguousarray(np.asarray(a, dtype=np.float32))
    shared = {
        "norm_ffn": f(norm_ffn), "ffn_w_in": f(ffn_w_in), "ffn_w_out": f(ffn_w_out), "norm_mix": f(norm_mix),
        "conv_w_in": f(conv_w_in)[0], "conv_w": f(conv_w)[0], "conv_w_out": f(conv_w_out)[0],
        "attn_w_in": f(attn_w_in)[0], "attn_b_f": f(attn_b_f)[0], "q_norm": f(q_norm)[0], "k_norm": f(k_norm)[0],
        "attn_w_out": f(attn_w_out)[0],
    }
    xp = f(x_prompt); xs = f(x_sample); sc = f(state_conv); ck = f(cache_k); cv = f(cache_v); cl = f(cache_logf)
    in_maps = []
    for b in range(NCORES):
        m = dict(shared)
        m.update({"xp": xp[b], "xs": xs[b], "sconv": sc[0, b], "ck": ck[0, b].reshape(PL, D), "cv": cv[0, b].reshape(PL, D),
                  "clf": cl[0, b]})
        in_maps.append(m)
    res = run_bass_kernel_spmd(nc, in_maps, core_ids=list(range(NCORES)))
    R = res.results
    st = lambda k: np.stack([np.asarray(R[b][k], dtype=np.float32) for b in range(NCORES)])
    yp = st("yp"); ys = st("ys")
    csp = st("csp")[None]; css = st("css")[None]
    kp = st("kp").reshape(1, NCORES, PL, H, HD); vp = st("vp").reshape(1, NCORES, PL, H, HD); lfp = st("lfp")[None]
    ks = st("ks").reshape(1, NCORES, SL, H, HD); vs = st("vs").reshape(1, NCORES, SL, H, HD); lfs = st("lfs")[None]
    return (yp, ys, csp, css, kp, vp, lfp, ks, vs, lfs)
```

```python
from contextlib import ExitStack
import numpy as np
import concourse.bass as bass
import concourse.mybir as mybir
from concourse.bass_utils import run_bass_kernel_spmd

F32 = mybir.dt.float32
BF16 = mybir.dt.bfloat16
AF = mybir.ActivationFunctionType
ALU = mybir.AluOpType
AX = mybir.AxisListType

D = 1024
KC = 8
FF = 2816
H = 16
HD = 64
PL = 2048
SL = 64
NT = 1088
EPS = 1e-6
NCORES = 8


class Trk:
    def __init__(self, nc, es):
        self.nc = nc
        self.engs = {"pe": nc.tensor, "act": nc.scalar, "dve": nc.vector, "pool": nc.gpsimd, "sp": nc.sync}
        self.sem = {k: es.enter_context(nc.semaphore("s_" + k)) for k in self.engs}
        self.cnt = {k: 0 for k in self.engs}
        self.known = {k: {} for k in self.engs}
        self.res = {}
        self.dsems = {q: [es.enter_context(nc.semaphore("d_%s%d" % (q, i))) for i in range(n)]
                      for q, n in (("sp", 14), ("pool", 14))}
        self.dval = {q: [0] * len(v) for q, v in self.dsems.items()}
        self.dnext = {q: 0 for q in self.dsems}
        self.outs = []

    def _waits(self, eng, reads, writes):
        waits = {}

        def need(ev, same_ok):
            if ev is None:
                return
            sem, val, e = ev
            if e == eng and same_ok:
                return
            k = id(sem)
            if self.known[eng].get(k, 0) >= val:
                return
            if k not in waits or waits[k][1] < val:
                waits[k] = (sem, val)

        for r in reads:
            st = self.res.get(r)
            if st:
                need(st[0], False)
                if isinstance(r, tuple) and r[0] == "ps":
                    for ev in st[1].values():
                        need(ev, True)
        for w in writes:
            st = self.res.get(w)
            if st:
                need(st[0], True)
                for ev in st[1].values():
                    need(ev, True)
        for k, (sem, val) in waits.items():
            self.engs[eng].wait_ge(sem, val)
            self.known[eng][k] = val

    def _record(self, ev, who, reads, writes):
        for r in reads:
            st = self.res.setdefault(r, [None, {}])
            st[1][who] = ev
        for w in writes:
            self.res[w] = [ev, {}]

    def op(self, eng, fn, reads=(), writes=()):
        self._waits(eng, reads, writes)
        ins = fn(self.engs[eng])
        self.cnt[eng] += 1
        ins.then_inc(self.sem[eng], 1)
        ev = (self.sem[eng], self.cnt[eng], eng)
        self._record(ev, eng, reads, writes)
        return ev

    def dma(self, q, out, in_, reads=(), writes=(), is_out=False, **kw):
        i = self.dnext[q]
        self.dnext[q] = (i + 1) % len(self.dsems[q])
        sem = self.dsems[q][i]
        prev = self.dval[q][i]
        if prev > 0 and self.known[q].get(id(sem), 0) < prev:
            self.engs[q].wait_ge(sem, prev)
            self.known[q][id(sem)] = prev
        self._waits(q, reads, writes)
        ins = self.engs[q].dma_start(out=out, in_=in_, **kw)
        ins.then_inc(sem, 16)
        self.dval[q][i] = prev + 16
        ev = (sem, prev + 16, "dma_%s_%d_%d" % (q, i, prev))
        self._record(ev, ev[2], reads, writes)
        if is_out:
            self.outs.append(ev)
        return ev

    def reset_for_write(self, eng, keys):
        self._waits(eng, [], keys)
        for k in keys:
            self.res[k] = [None, {}]

    def finish(self):
        for sem, val, _ in self.outs:
            if self.known["sp"].get(id(sem), 0) < val:
                self.engs["sp"].wait_ge(sem, val)
                self.known["sp"][id(sem)] = val


class Ring:
    def __init__(self, nslots, plan):
        self.n = nslots
        self.plan = plan
        self.nl = 0
        self.nu = 0

    def prefetch(self):
        while self.nl < len(self.plan) and self.nl < self.nu + self.n:
            self.plan[self.nl](self.nl % self.n)
            self.nl += 1

    def acquire(self, off=0):
        self.prefetch()
        assert self.nl > self.nu + off
        return (self.nu + off) % self.n

    def release(self):
        self.nu += 1
        self.prefetch()


def build_nc():
    nc = bass.Bass("TRN2", target_bir_lowering=False)
    dt = lambda n, s, k="ExternalInput": nc.dram_tensor(n, s, F32, kind=k).ap()
    xp_d = dt("xp", [PL, D]); xs_d = dt("xs", [SL, D]); sconv_d = dt("sconv", [2, D])
    ck_d = dt("ck", [PL, D]); cv_d = dt("cv", [PL, D]); clf_d = dt("clf", [PL, H])
    nffn_d = dt("norm_ffn", [2, 2, D]); wfi_d = dt("ffn_w_in", [2, 2, D, 2 * FF]); wfo_d = dt("ffn_w_out", [2, 2, FF, D])
    nmix_d = dt("norm_mix", [2, D]); cwi_d = dt("conv_w_in", [D, 3 * D]); cw_d = dt("conv_w", [3, D])
    cwo_d = dt("conv_w_out", [D, D]); awi_d = dt("attn_w_in", [D, 3 * D + H]); abf_d = dt("attn_b_f", [H])
    qn_d = dt("q_norm", [HD]); kn_d = dt("k_norm", [HD]); awo_d = dt("attn_w_out", [D, D])
    O = "ExternalOutput"
    yp_d = dt("yp", [PL, D], O); ys_d = dt("ys", [SL, D], O); csp_d = dt("csp", [2, D], O); css_d = dt("css", [2, D], O)
    kp_d = dt("kp", [PL, D], O); vp_d = dt("vp", [PL, D], O); lfp_d = dt("lfp", [PL, H], O)
    ks_d = dt("ks", [SL, D], O); vs_d = dt("vs", [SL, D], O); lfs_d = dt("lfs", [SL, H], O)

    with ExitStack() as es:
        sb = lambda n, s, d=F32: es.enter_context(nc.sbuf_tensor(n, s, d))
        x = sb("x", [128, 9, D])
        xnT = sb("xnT", [128, KC, NT], BF16)
        mT = sb("mT", [128, KC, NT], BF16)
        gT = sb("gT", [128, 12 * NT], BF16)
        Wo = sb("Wo", [128, 12, D], BF16)
        ringb = [sb("ring%d" % i, [128, 4096], BF16) for i in range(3)]
        gtile = [sb("gtile%d" % i, [128, D]) for i in range(2)]
        xns = [sb("xns%d" % i, [128, D], BF16) for i in range(3)]
        sq = sb("sq", [128, 512])
        stmp = [sb("stmp%d" % i, [128, 512]) for i in range(3)]
        ub = [sb("ub%d" % i, [128, 516]) for i in range(2)]
        tA = sb("tA", [128, 512])
        kTr = [sb("kTr%d" % i, [128, 17 * 128], BF16) for i in range(2)]
        qTr = [sb("qTr%d" % i, [128, NT], BF16) for i in range(2)]
        PT = [sb("PT%d" % i, [128, 512], BF16) for i in range(3)]
        Otok = sb("Otok", [128, 9, 256], BF16)
        kst = [sb("kst%d" % i, [128, 256]) for i in range(2)]
        vst = [sb("vst%d" % i, [128, 256]) for i in range(2)]
        small = [sb("small%d" % i, [128, 64]) for i in range(6)]
        lfst = [sb("lfst%d" % i, [128, 16]) for i in range(5)]
        c_p = sb("c_p", [128, 16, 16]); nb_p = sb("nb_p", [128, 16, 16])
        c_s = sb("c_s", [128, 17, 16]); nb_s = sb("nb_s", [128, 17, 16])
        cs_p = sb("cs_p", [128, 16, 16, 3], BF16); cs_s = sb("cs_s", [128, 1, 16, 3], BF16)
        lfc = sb("lfc", [128, 16, 16])
        R_p = sb("R_p", [128, 16]); R_s = sb("R_s", [128, 16])
        ident = sb("ident", [128, 128], BF16); tri = sb("tri", [128, 128], BF16)
        identf = sb("identf", [128, 128]); U = sb("U", [128, 128]); ones = sb("ones", [128, 128])
        epsT = sb("epsT", [128, 1]); bfT = sb("bfT", [128, 16]); gq8 = sb("gq8", [128, 64]); gkT = sb("gkT", [128, 64])
        negM16 = sb("negM16", [128, 16]); mtmp = sb("mtmp", [128, 8])
        histp = sb("histp", [128, 2, 8]); hsin = sb("hsin", [128, 2, 8]); hsout = sb("hsout", [128, 2, 8])
        wk = sb("wk", [128, 3, 8]); oneT = sb("oneT", [128, 1])
        ps = [es.enter_context(nc.psum_tensor("ps%d" % i, [128, 512], F32)) for i in range(8)]
        T = Trk(nc, es)

        KG0 = 0; VG0 = 17 * 256; QG0 = VG0 + 17 * 4 * 66
        Kg = lambda: gT[:, KG0:KG0 + 17 * 256].rearrange("p (t c) -> p t c", c=256)
        Vg = lambda: gT[:, VG0:VG0 + 17 * 264].rearrange("p (t h c) -> p t h c", h=4, c=66)
        Qg = lambda: gT[:, QG0:QG0 + 9 * 272].rearrange("p (t h c) -> p t h c", h=4, c=68)
        gTv = lambda: gT[:, :].rearrange("p (f n) -> p f n", n=NT)
        Kc = Wo[:, 8:12, :].rearrange("p f (t c) -> p (f t) c", c=256)

        st = {"ta": 0, "pb": 0, "ev": 0, "ss": 0, "stmp": 0, "xns": 0, "ub": 0, "pt": 0, "ob": 0, "kst": 0, "lf": 0, "ktr": 0, "qtr": 0}

        def rr(name, n):
            v = st[name]
            st[name] = (v + 1) % n
            return v

        st["npb"] = 8

        def pbank():
            v = st["pb"] % st["npb"]
            st["pb"] = (v + 1) % st["npb"]
            return v
        eveng = lambda: ("act", "dve")[rr("ev", 2)]

        def copy_op(eng, out, in_, reads, writes):
            if eng == "act":
                return T.op("act", lambda e: e.copy(out=out, in_=in_), reads, writes)
            return T.op(eng, lambda e: e.tensor_copy(out=out, in_=in_), reads, writes)

        def mm_group(out_ap, pairs, reads, writes):
            def fn(pe):
                n = len(pairs)
                for i, (l, r) in enumerate(pairs):
                    ins = pe.matmul(out_ap, l, r, start=(i == 0), stop=(i == n - 1))
                return ins
            return T.op("pe", fn, reads, writes)

        prologue_q = []
        T.op("pool", lambda e: e.memset(identf[:], 0.0), [], ["identf"])
        T.op("pool", lambda e: e.affine_select(out=identf[:], in_=identf[:], pattern=[[-1, 128]], compare_op=ALU.not_equal,
                                               fill=1.0, base=0, channel_multiplier=1), ["identf"], ["identf"])
        T.op("pool", lambda e: e.memset(epsT[:], EPS), [], ["eps"])
        T.op("pool", lambda e: e.tensor_copy(out=ident[:], in_=identf[:]), ["identf"], ["ident"])

        def consts_late():
            T.op("pool", lambda e: e.memset(U[:], 1.0), [], ["U"])
            T.op("pool", lambda e: e.affine_select(out=U[:], in_=U[:], pattern=[[1, 128]], compare_op=ALU.is_ge,
                                                   fill=0.0, base=0, channel_multiplier=-1), ["U"], ["U"])
            T.op("pool", lambda e: e.memset(ones[:], 1.0), [], ["ones"])
            T.op("pool", lambda e: e.memset(oneT[:], 1.0), [], ["oneT"])
            T.op("pool", lambda e: e.tensor_copy(out=tri[:], in_=U[:]), ["U"], ["tri"])
            T.op("pool", lambda e: e.memset(R_p[:], 0.0), [], ["R_p"])
            T.op("pool", lambda e: e.memset(R_s[:], 0.0), [], ["R_s"])
            T.op("pool", lambda e: e.memset(histp[:], 0.0), [], ["histp"])
            for i in range(2):
                T.op("pool", lambda e, i=i: e.memset(kTr[i][:], 1.0), [], [("kTr", i, j) for j in range(5)])
            T.dma("sp", bfT[:], abf_d.partition_broadcast(128), [], ["bfT"])
            T.dma("sp", gq8[:], qn_d.partition_broadcast(128), [], ["gq8"])
            T.dma("sp", gkT[:], kn_d.partition_broadcast(128), [], ["gkT"])
            for j in range(3):
                T.dma("sp", wk[:, j, :], cw_d[j].rearrange("(dc p) -> p dc", p=128), [], [("wk", j)], allow_slow_non_contiguous=True)
            for t in range(2):
                T.dma("sp", hsin[:, t, :], sconv_d[t].rearrange("(dc p) -> p dc", p=128), [], [("hsin", t)], allow_slow_non_contiguous=True)
            T.dma("sp", lfc[:], clf_d.rearrange("(t p) h -> p t h", p=128), [], ["lfc"])
            def m_compute():
                sa_ = rr("ss", 6); sb_ = rr("ss", 6)
                ka_ = ("sm", sa_); kb_ = ("sm", sb_)
                T.op("dve", lambda e: e.tensor_tensor(out=small[sa_][:, 0:64], in0=gq8[:], in1=gq8[:], op=ALU.mult), ["gq8"], [ka_])
                T.op("dve", lambda e: e.reduce_max(out=mtmp[:, 0:1], in_=small[sa_][:, 0:64], axis=AX.X), [ka_], ["mtmp"])
                T.op("dve", lambda e: e.tensor_tensor(out=small[sb_][:, 0:64], in0=gkT[:], in1=gkT[:], op=ALU.mult), ["gkT"], [kb_])
                T.op("dve", lambda e: e.reduce_max(out=mtmp[:, 1:2], in_=small[sb_][:, 0:64], axis=AX.X), [kb_], ["mtmp"])
                T.op("dve", lambda e: e.tensor_tensor(out=mtmp[:, 2:3], in0=mtmp[:, 0:1], in1=mtmp[:, 1:2], op=ALU.mult), ["mtmp"], ["mtmp"])
                T.op("act", lambda e: e.activation(out=mtmp[:, 3:4], in_=mtmp[:, 2:3], func=AF.Ln, scale=64.0), ["mtmp"], ["mtmp2"])
                T.op("act", lambda e: e.activation(out=mtmp[:, 4:5], in_=mtmp[:, 3:4], func=AF.Exp, scale=0.5), ["mtmp2"], ["mtmp3"])
                T.op("dve", lambda e: e.tensor_scalar(out=negM16[:], in0=ones[:, 0:16], scalar1=mtmp[:, 4:5], scalar2=None,
                                                      op0=ALU.mult), ["mtmp3", "ones"], ["negM16"])
                T.op("dve", lambda e: e.tensor_scalar(out=negM16[:], in0=negM16[:], scalar1=-1.0, scalar2=None,
                                                      op0=ALU.mult), ["negM16"], ["negM16"])
                T.op("dve", lambda e: e.tensor_scalar(out=gq8[:], in0=gq8[:], scalar1=0.125, scalar2=None, op0=ALU.mult), ["gq8", ka_], ["gq8"])


            prologue_q.append(m_compute)

        def wview(w2d, c0, n):
            return w2d[:, c0:c0 + n].rearrange("(kc p) n -> p kc n", p=128)

        ring_plan = []
        wo_plan = []
        RK = lambda slot: [("ring", slot, 0), ("ring", slot, 1), ("ring", slot, 2)]
        WK = [("wo", 0), ("wo", 1)]

        def ld_ffn_in(i, j, t):
            def f(slot):
                T.reset_for_write("pool", RK(slot))
                v = ringb[slot][:, :].rearrange("p (kc a n) -> p kc a n", kc=8, a=2)
                T.dma("pool", v[:, :, 0, :], wview(wfi_d[i, j], 256 * t, 256), [], [("ring", slot, 0)])
                T.dma("pool", v[:, :, 1, :], wview(wfi_d[i, j], FF + 256 * t, 256), [], [("ring", slot, 1)])
            return f

        def ld_conv_in(dc):
            def f(slot):
                T.reset_for_write("pool", RK(slot))
                v = ringb[slot][:, 0:3072].rearrange("p (kc a n) -> p kc a n", kc=8, a=3)
                for a in range(3):
                    T.dma("pool", v[:, :, a, :], wview(cwi_d, a * D + dc * 128, 128), [], [("ring", slot, a)])
            return f

        def ld_attn_a(g):
            def f(slot):
                T.reset_for_write("pool", RK(slot))
                v = ringb[slot][:, :].rearrange("p (kc a n) -> p kc a n", kc=8, a=2)
                T.dma("pool", v[:, :, 0, :], wview(awi_d, g * 256, 256), [], [("ring", slot, 0)])
                T.dma("pool", v[:, :, 1, :], wview(awi_d, D + g * 256, 256), [], [("ring", slot, 1)])
            return f

        def ld_attn_b(g):
            def f(slot):
                T.reset_for_write("pool", RK(slot))
                v = ringb[slot][:, 0:8 * 272].rearrange("p (kc n) -> p kc n", kc=8)
                T.dma("pool", v[:, :, 0:256], wview(awi_d, 2 * D + g * 256, 256), [], [("ring", slot, 0)])
                if g == 0:
                    T.dma("pool", v[:, :, 256:272], wview(awi_d, 3 * D, 16), [], [("ring", slot, 1)])
            return f

        def ld_wo(src2d, r0, nfc):
            def f(slot):
                T.reset_for_write("pool", WK + ["Kc"])
                h1 = nfc // 2
                for pi, (a, b_) in enumerate(((0, h1), (h1, nfc))):
                    T.dma("pool", Wo[:, a:b_, :], src2d[r0 + a * 128:r0 + b_ * 128, :].rearrange("(f p) n -> p f n", p=128),
                          [], [("wo", pi)])
            return f

        FPARTS = ((0, 12), (12, 10))
        for h in range(2):
            for i in range(2):
                for j in range(2):
                    for (f0, nf) in FPARTS:
                        for t in range(f0 // 2, (f0 + nf) // 2):
                            ring_plan.append(ld_ffn_in(i, j, t))
                        wo_plan.append(ld_wo(wfo_d[i, j], f0 * 128, nf))
                    if j == 0:
                        if i == 0:
                            for dc in range(8):
                                ring_plan.append(ld_conv_in(dc))
                            wo_plan.append(ld_wo(cwo_d, 0, 8))
                        else:
                            for g in range(4):
                                ring_plan.append(ld_attn_a(g))
                                ring_plan.append(ld_attn_b(g))
                            wo_plan.append(ld_wo(awo_d, 0, 8))
        ring = Ring(3, ring_plan)
        wor = Ring(1, wo_plan)

        def tiles(h):
            return [(lt, 128) for lt in range(8)] + ([(8, 64)] if h == 1 else [])

        def groups(h):
            g = [(0, 512, [0, 1, 2, 3]), (512, 512, [4, 5, 6, 7])]
            if h == 1:
                g.append((1024, 64, [8]))
            return g

        tg_of = lambda lt: 2 if lt == 8 else lt // 4

        pending = []
        norm_q = []
        reload_q = []

        def do_reload(a):
            lt, rows, gk_ = a
            T.dma("sp", x[:, lt, :], xp_d[(8 + lt) * 128:(8 + lt + 1) * 128, :], [], [("x", lt)])
            norm_q.append(a)
            while len(norm_q) > 2:
                b_ = norm_q.pop(0)
                flush_pending(keep=2)
                norm_tile(*b_)

        def need_xnT(lt):
            while any(p[0] == lt for p in pending):
                p = pending.pop(0)
                p[1]()

        def flush_pending(keep=0):
            while len(pending) > keep:
                pending.pop(0)[1]()

        st["g"] = 0

        def start_norm(vec_ap):
            k = rr("g", 2)
            T.dma("sp", gtile[k][:], vec_ap.partition_broadcast(128), [], [("gtile", k)])
            return k

        def norm_tile(lt, rows, gk):
            si = rr("ss", 6)
            sm = small[si]
            T.op("act", lambda e: e.activation(out=sq[:rows, :].bitcast(BF16), in_=x[:rows, lt, :], func=AF.Square,
                                               accum_out=sm[:rows, 0:1]), [("x", lt)], ["sq", ("sm", si)])
            T.op("act", lambda e: e.activation(out=sm[:rows, 1:2], in_=sm[:rows, 0:1], func=AF.Ln, scale=1.0 / D,
                                               bias=epsT[:rows, 0:1]), [("sm", si), "eps"], [("sm", si)])
            T.op("act", lambda e: e.activation(out=sm[:rows, 2:3], in_=sm[:rows, 1:2], func=AF.Exp, scale=-0.5),
                 [("sm", si)], [("sm", si)])
            xi = rr("xns", 7)
            if xi < 3:
                xb = xns[xi]; xks = [("xns", xi)]
            elif xi < 5:
                xb = qTr[xi - 3][:, 0:1024]; xks = [("qTr", xi - 3, j) for j in range(3)]
            else:
                o4 = (xi - 5) * 4
                xb = Otok[:, o4:o4 + 4, :].rearrange("p t c -> p (t c)"); xks = [("Otok", o4 + j) for j in range(4)]
            T.op("dve", lambda e: e.scalar_tensor_tensor(out=xb[:rows, :], in0=x[:rows, lt, :], scalar=sm[:rows, 2:3],
                                                         in1=gtile[gk][:rows, :], op0=ALU.mult, op1=ALU.mult),
                 [("x", lt), ("sm", si), ("gtile", gk)], xks)

            def stage_b():
                b = pbank()
                psb = ps[b][:, :].bitcast(BF16)

                def fn(pe):
                    for kc in range(8):
                        ins = pe.transpose(psb[:, kc * 128:kc * 128 + rows], xb[:rows, kc * 128:(kc + 1) * 128],
                                           ident[:rows, :rows])
                    return ins
                T.op("pe", fn, xks + ["ident"], [("ps", b)])
                copy_op(eveng(), xnT[:, :, lt * 128:lt * 128 + rows],
                        psb.rearrange("p (k n) -> p k n", n=128)[:, :, 0:rows], [("ps", b)], [("xnT", lt)])
            pending.append((lt, stage_b))

        def after_tile(post, lt, rows):
            if post is None:
                return
            if post[0] == "norm":
                flush_pending(keep=2)
                norm_tile(lt, rows, post[1])
            else:
                h = post[1]
                if lt < 8:
                    T.dma("sp", yp_d[(h * 8 + lt) * 128:(h * 8 + lt + 1) * 128, :], x[:, lt, :], [("x", lt)], [], is_out=True)
                    if h == 0:
                        reload_q.append((lt, rows, post[2]))
                        while len(reload_q) > 1:
                            do_reload(reload_q.pop(0))
                else:
                    T.dma("sp", ys_d[:, :], x[:64, lt, :], [("x", lt)], [], is_out=True)

        def ffn(h, post):
            gv = gTv()
            for pi, (f0, nf) in enumerate(FPARTS):
                for t in range(nf // 2):
                    slot = ring.acquire()
                    wv = ringb[slot][:, :].rearrange("p (kc a n) -> p kc a n", kc=8, a=2)
                    for (c0, n, tl) in groups(h):
                        for lt in tl:
                            need_xnT(lt)
                        xk = [("xnT", lt) for lt in tl]
                        for sub in range(2):
                            fcl = 2 * t + sub
                            bg = pbank(); bu = pbank()
                            mm_group(ps[bg][:, 0:n], [(wv[:, kc, 0, sub * 128:(sub + 1) * 128], xnT[:, kc, c0:c0 + n]) for kc in range(8)],
                                     RK(slot) + xk, [("ps", bg)])
                            mm_group(ps[bu][:, 0:n], [(wv[:, kc, 1, sub * 128:(sub + 1) * 128], xnT[:, kc, c0:c0 + n]) for kc in range(8)],
                                     RK(slot) + xk, [("ps", bu)])
                            si = rr("stmp", 3)
                            T.op("act", lambda e: e.activation(out=stmp[si][:, 0:n], in_=ps[bg][:, 0:n], func=AF.Silu),
                                 [("ps", bg)], [("stmp", si)])
                            T.op("dve", lambda e: e.tensor_tensor(out=gv[:, fcl, c0:c0 + n], in0=stmp[si][:, 0:n], in1=ps[bu][:, 0:n],
                                                                  op=ALU.mult), [("stmp", si), ("ps", bu)], [("gT", fcl, c0)])
                    ring.release()
                    if prologue_q:
                        prologue_q.pop(0)()
                wor.acquire()
                for (lt, rows) in tiles(h):
                    c0g = groups(h)[tg_of(lt)][0]
                    for ch in range(2):
                        b = pbank()
                        mm_group(ps[b][:rows, :], [(gv[:, fcl, lt * 128:lt * 128 + rows], Wo[:, fcl, ch * 512:(ch + 1) * 512]) for fcl in range(nf)],
                                 WK + [("gT", fcl, c0g) for fcl in range(nf)], [("ps", b)])
                        T.op("dve", lambda e: e.scalar_tensor_tensor(out=x[:rows, lt, ch * 512:(ch + 1) * 512], in0=ps[b][:rows, :], scalar=0.5,
                                                                     in1=x[:rows, lt, ch * 512:(ch + 1) * 512], op0=ALU.mult, op1=ALU.add),
                             [("ps", b), ("x", lt)], [("x", lt)])
                    if pi == 1:
                        after_tile(post, lt, rows)
                wor.release()

        def out_proj(h, post):
            wor.acquire()
            for (lt, rows) in tiles(h):
                for ch in range(2):
                    b = pbank()
                    mm_group(ps[b][:rows, :], [(mT[:, dc, lt * 128:lt * 128 + rows], Wo[:, dc, ch * 512:(ch + 1) * 512]) for dc in range(8)],
                             WK + [("mT", lt)], [("ps", b)])
                    T.op("dve", lambda e: e.tensor_tensor(out=x[:rows, lt, ch * 512:(ch + 1) * 512], in0=ps[b][:rows, :],
                                                          in1=x[:rows, lt, ch * 512:(ch + 1) * 512], op=ALU.add),
                         [("ps", b), ("x", lt)], [("x", lt)])
                after_tile(post, lt, rows)
            wor.release()

        WKK = [("wk", 0), ("wk", 1), ("wk", 2)]

        def conv(h):
            for dc in range(8):
                slot = ring.acquire()
                wv = ringb[slot][:, 0:3072].rearrange("p (kc a n) -> p kc a n", kc=8, a=3)
                prev_ub = None
                gl = list(enumerate(groups(h)))
                if h == 1:
                    gl = [gl[2], gl[0], gl[1]]
                for gi, (c0, n, tl) in gl:
                    for lt in tl:
                        need_xnT(lt)
                    xk = [("xnT", lt) for lt in tl]
                    bb = [pbank() for _ in range(3)]
                    for a in range(3):
                        mm_group(ps[bb[a]][:, 0:n], [(wv[:, kc, a, :], xnT[:, kc, c0:c0 + n]) for kc in range(8)],
                                 RK(slot) + xk, [("ps", bb[a])])
                    ui = rr("ub", 2)
                    u = ub[ui]
                    if gi == 2:
                        copy_op("act", u[:, 0:2], hsin[:, :, dc], [("hsin", 0), ("hsin", 1)], [("ub", ui)])
                    elif gi == 0:
                        copy_op("act", u[:, 0:2], histp[:, :, dc], ["histp"], [("ub", ui)])
                    else:
                        copy_op("act", u[:, 0:2], prev_ub[0][:, prev_ub[1]:prev_ub[1] + 2], [("ub", prev_ub[2])], [("ub", ui)])
                    si = rr("stmp", 3)
                    copy_op("act", stmp[si][:, 0:n], ps[bb[1]][:, 0:n], [("ps", bb[1])], [("stmp", si)])
                    T.op("dve", lambda e: e.tensor_tensor(out=u[:, 2:2 + n], in0=stmp[si][:, 0:n], in1=ps[bb[2]][:, 0:n], op=ALU.mult),
                         [("stmp", si), ("ps", bb[2])], [("ub", ui)])
                    tAb, tAk = ((tA, "tA"), (sq, "sq"))[rr("ta", 2)]
                    T.op("act", lambda e: e.activation(out=tAb[:, 0:n], in_=u[:, 0:n], func=AF.Copy, scale=wk[:, 0, dc:dc + 1]),
                         [("ub", ui)] + WKK, [tAk])
                    T.op("dve", lambda e: e.scalar_tensor_tensor(out=tAb[:, 0:n], in0=u[:, 1:1 + n], scalar=wk[:, 1, dc:dc + 1], in1=tAb[:, 0:n],
                                                                 op0=ALU.mult, op1=ALU.add), [("ub", ui), tAk] + WKK, [tAk])
                    T.op("dve", lambda e: e.scalar_tensor_tensor(out=tAb[:, 0:n], in0=u[:, 2:2 + n], scalar=wk[:, 2, dc:dc + 1], in1=tAb[:, 0:n],
                                                                 op0=ALU.mult, op1=ALU.add), [("ub", ui), tAk] + WKK, [tAk])
                    T.op("dve", lambda e: e.tensor_tensor(out=mT[:, dc, c0:c0 + n], in0=tAb[:, 0:n], in1=ps[bb[0]][:, 0:n], op=ALU.mult),
                         [tAk, ("ps", bb[0])], [("mT", lt) for lt in tl])
                    if gi == 1:
                        copy_op("act", histp[:, :, dc], u[:, n:n + 2], [("ub", ui)], ["histp"])
                    if gi == 2:
                        copy_op("act", hsout[:, :, dc], u[:, n:n + 2], [("ub", ui)], ["hsout"])
                    if gi != 2:
                        prev_ub = (u, n, ui)
                ring.release()

        def cumsum_tile(lf_ap, rows, Rt, Rkey, c_ap, ckey, lfkeys):
            b = pbank()

            def fn(pe):
                pe.matmul(ps[b][:rows, 0:16], U[:rows, :rows], lf_ap, start=True, stop=False)
                return pe.matmul(ps[b][:rows, 0:16], ones[:, :rows], Rt[:, :], start=False, stop=True)
            T.op("pe", fn, lfkeys + [Rkey, "U", "ones"], [("ps", b)])
            T.op("dve", lambda e: e.tensor_copy(out=c_ap, in_=ps[b][:rows, 0:16]), [("ps", b)], [ckey])
            T.op("dve", lambda e: e.tensor_tensor(out=Rt[:rows, :], in0=Rt[:rows, :], in1=lf_ap, op=ALU.add), lfkeys + [Rkey], [Rkey])

        def cumsum_post(rows, nt_, c_v, nb_v, ckeys, cs_v=None):
            T.op("dve", lambda e: e.scalar_tensor_tensor(out=nb_v, in0=c_v, scalar=-1.0,
                                                         in1=negM16[:rows, :].unsqueeze(1).to_broadcast([rows, nt_, 16]), op0=ALU.mult, op1=ALU.add),
                 ckeys + ["negM16"], [(k, "nb") for k in ckeys])
            if cs_v is not None:
                si = rr("stmp", 3)
                n_ = nt_ * 16
                r1 = stmp[si][:rows, 0:n_].rearrange("p (t h) -> p t h", h=16)
                r2 = stmp[si][:rows, 128:128 + n_].rearrange("p (t h) -> p t h", h=16)
                k = ("stmp", si)
                T.op("dve", lambda e: e.tensor_copy(out=cs_v[:, :, :, 0], in_=c_v), ckeys, [(kk, "cs") for kk in ckeys])
                T.op("dve", lambda e: e.tensor_tensor(out=r1, in0=c_v, in1=cs_v[:, :, :, 0], op=ALU.subtract), ckeys + [(kk, "cs") for kk in ckeys], [k])
                T.op("dve", lambda e: e.tensor_copy(out=cs_v[:, :, :, 1], in_=r1), [k], [(kk, "cs1") for kk in ckeys])
                T.op("dve", lambda e: e.tensor_tensor(out=r2, in0=r1, in1=cs_v[:, :, :, 1], op=ALU.subtract), [k] + [(kk, "cs1") for kk in ckeys], [(k, 2)])
                T.op("dve", lambda e: e.tensor_copy(out=cs_v[:, :, :, 2], in_=r2), [(k, 2), k], [(kk, "cs2") for kk in ckeys])

        def attn_job(g, q_tiles, key_tiles, nb, cname, batch=False):
            Kv = Kg(); Vv = Vg(); Qv = Qg()
            nk = len(key_tiles)
            LA = 3
            slots = {}
            PTs = PT + [tA[:, :].bitcast(BF16)]
            PTk = [("PT", 0), ("PT", 1), ("PT", 2), "tA"]

            def emit_transposes(hh):
                ks = rr("ktr", 2); qs = rr("qtr", 2)
                slots[hh] = (ks, qs)
                for b0 in range(0, nk, 4):
                    chunk = key_tiles[b0:b0 + 4]
                    b = pbank()
                    psb = ps[b][:, :].bitcast(BF16)

                    def fn(pe, chunk=chunk, psb=psb):
                        for j, (idx, rows, _, own) in enumerate(chunk):
                            src = Kc if (batch and own is None) else Kv
                            ins = pe.transpose(psb[0:64, j * 128:j * 128 + rows], src[:rows, idx, hh * 64:(hh + 1) * 64], ident[:rows, :rows])
                        return ins
                    T.op("pe", fn, [("Kc" if (batch and own is None) else ("Kg", idx)) for (idx, _, _, own) in chunk] + ["ident"], [("ps", b)])
                    ncol = (len(chunk) - 1) * 128 + chunk[-1][1]
                    copy_op("dve", kTr[ks][0:64, b0 * 128:b0 * 128 + ncol], psb[0:64, 0:ncol], [("ps", b)], [("kTr", ks, b0 // 4)])
                for b0 in range(0, len(q_tiles), 4):
                    chunk = q_tiles[b0:b0 + 4]
                    b = pbank()
                    psb = ps[b][:, :].bitcast(BF16)

                    def fn(pe, chunk=chunk, psb=psb):
                        for j, (lt, rows) in enumerate(chunk):
                            ins = pe.transpose(psb[0:67, j * 128:j * 128 + rows], Qv[:rows, lt, hh, 0:67], ident[:rows, :rows])
                        return ins
                    T.op("pe", fn, [("Qg", lt) for (lt, _) in chunk] + ["ident"], [("ps", b)])
                    ncol = (len(chunk) - 1) * 128 + chunk[-1][1]
                    copy_op("dve", qTr[qs][0:67, b0 * 128:b0 * 128 + ncol], psb[0:67, 0:ncol], [("ps", b)], [("qTr", qs, b0 // 4)])

            steps = []
            for hh in range(4):
                for q0 in range(0, len(q_tiles), 4):
                    qt = q_tiles[q0:q0 + 4]
                    last_pos = q0 + len(qt) - 1
                    rel = [kt for kt in key_tiles if kt[3] is None or kt[3] <= last_pos]
                    grp = {"ob": None}
                    if batch:
                        pk_ = [kt for kt in rel if kt[3] is None]
                        ow_ = [kt for kt in rel if kt[3] is not None]
                        ki_ = 0
                        for b0 in range(0, len(pk_), 8):
                            steps.append({"hh": hh, "q0": q0, "qt": qt, "ki": ki_, "kt": pk_[b0], "batch": pk_[b0:b0 + 8], "last": False,
                                          "grp": grp, "newhead": (q0 == 0 and ki_ == 0)})
                            ki_ += 1
                        for kt in ow_:
                            steps.append({"hh": hh, "q0": q0, "qt": qt, "ki": ki_, "kt": kt, "batch": None, "last": kt is ow_[-1], "grp": grp,
                                          "newhead": False})
                            ki_ += 1
                        continue
                    for ki_, kt in enumerate(rel):
                        steps.append({"hh": hh, "q0": q0, "qt": qt, "ki": ki_, "kt": kt, "batch": None, "last": ki_ == len(rel) - 1, "grp": grp,
                                      "newhead": (q0 == 0 and ki_ == 0)})

            def front(s):
                hh = s["hh"]; q0 = s["q0"]; qt = s["qt"]
                ks, qs = slots[hh]
                head = g * 4 + hh
                if s["batch"]:
                    nq = sum(r for _, r in qt)
                    b = pbank()
                    bl = s["batch"]

                    def fnq(pe):
                        for j, kt in enumerate(bl):
                            kcol = key_tiles.index(kt) * 128
                            ins = pe.matmul(ps[b][:, j * nq:(j + 1) * nq], kTr[ks][0:67, kcol:kcol + 128],
                                            qTr[qs][0:67, q0 * 128:q0 * 128 + nq], start=True, stop=True)
                        return ins
                    T.op("pe", fnq, [("kTr", ks, key_tiles.index(kt) // 4) for kt in bl] + [("qTr", qs, q0 // 4)], [("ps", b)])
                    pi = rr("pt", 4)
                    T.op("act", lambda e: e.activation(out=PTs[pi][:, 0:len(bl) * nq], in_=ps[b][:, 0:len(bl) * nq], func=AF.Exp,
                                                       bias=negM16[:, 0:1], scale=1.0), [("ps", b), "negM16"], [PTk[pi]])
                    s["pi"] = pi; s["s_t"] = 0
                    return
                idx, krows, nbidx, own_pos = s["kt"]
                nq = sum(r for _, r in qt)
                kcol = key_tiles.index(s["kt"]) * 128
                s_t = 0 if (own_pos is None or own_pos < q0) else own_pos - q0
                start_c = s_t * 128
                N = nq - start_c
                b = pbank()
                T.op("pe", lambda pe: pe.matmul(ps[b][:krows, 0:N], kTr[ks][0:67, kcol:kcol + krows],
                                                qTr[qs][0:67, q0 * 128 + start_c:q0 * 128 + nq], start=True, stop=True),
                     [("kTr", ks, kcol // 512), ("qTr", qs, q0 // 4)], [("ps", b)])
                pi = rr("pt", 4)
                T.op("act", lambda e: e.activation(out=PTs[pi][:krows, 0:N], in_=ps[b][:krows, 0:N], func=AF.Exp,
                                                   bias=nb[:krows, nbidx, head:head + 1], scale=1.0),
                     [("ps", b), ((cname, nbidx), "nb")], [PTk[pi]])
                if own_pos is not None and own_pos >= q0:
                    dq = qt[s_t][1]
                    T.op("dve", lambda e: e.tensor_tensor(out=PTs[pi][:krows, 0:dq], in0=PTs[pi][:krows, 0:dq], in1=tri[:krows, 0:dq],
                                                          op=ALU.mult), [PTk[pi], "tri"], [PTk[pi]])
                s["pi"] = pi; s["s_t"] = s_t

            def back(s):
                hh = s["hh"]; q0 = s["q0"]; qt = s["qt"]; ki_ = s["ki"]; pi = s["pi"]; s_t = s["s_t"]
                idx, krows, nbidx, own_pos = s["kt"]
                if s["grp"]["ob"] is None:
                    s["grp"]["ob"] = 6 + rr("ob", 2)
                ob = s["grp"]["ob"]

                if s["batch"]:
                    bl = s["batch"]
                    nq = sum(r for _, r in qt)

                    def fnb(pe):
                        for j, kt in enumerate(bl):
                            ins = pe.matmul(ps[ob][:nq, 0:65], PTs[pi][:, j * nq:(j + 1) * nq], Vv[:, kt[0], hh, 0:65],
                                            start=(ki_ == 0 and j == 0), stop=False, skip_group_check=True)
                        return ins
                    T.op("pe", fnb, [PTk[pi]] + [("Vg", kt[0], hh) for kt in bl], [("ps", ob)])
                    return

                def fn(pe):
                    for i in range(s_t, len(qt)):
                        qrows = qt[i][1]
                        first = (ki_ == 0 and i == 0)
                        lastk = (own_pos is not None and own_pos == q0 + i)
                        ins = pe.matmul(ps[ob][:qrows, i * 65:i * 65 + 65], PTs[pi][:krows, (i - s_t) * 128:(i - s_t) * 128 + qrows],
                                        Vv[:krows, idx, hh, 0:65], start=first, stop=lastk, skip_group_check=True)
                    return ins
                T.op("pe", fn, [PTk[pi], ("Vg", idx, hh)], [("ps", ob)])
                if s["last"]:
                    nt_ = len(qt)
                    qrows = qt[0][1]
                    lt0 = qt[0][0]
                    si = rr("ss", 6)
                    ov = ps[ob][:qrows, 0:nt_ * 65].rearrange("p (t c) -> p t c", c=65)
                    T.op("dve", lambda e: e.reciprocal(out=small[si][:qrows, 0:nt_], in_=ov[:, :, 64]), [("ps", ob)], [("sm", si)])
                    T.op("dve", lambda e: e.tensor_tensor(out=Otok[:qrows, lt0:lt0 + nt_, hh * 64:(hh + 1) * 64], in0=ov[:, :, 0:64],
                                                          in1=small[si][:qrows, 0:nt_].unsqueeze(2).to_broadcast([qrows, nt_, 64]), op=ALU.mult),
                         [("ps", ob), ("sm", si)], [("Otok", lt) for (lt, _) in qt])

            emit_transposes(0)
            for i in range(len(steps) + LA):
                if i < len(steps):
                    s = steps[i]
                    front(s)
                    if s["newhead"] and s["hh"] + 1 < 4:
                        emit_transposes(s["hh"] + 1)
                if i >= LA:
                    back(steps[i - LA])
            for (lt, rows) in q_tiles:
                b = pbank()
                psb = ps[b][:, :].bitcast(BF16)

                def fn(pe):
                    for j in range(2):
                        ins = pe.transpose(psb[:, j * 128:j * 128 + rows], Otok[:rows, lt, j * 128:(j + 1) * 128], ident[:rows, :rows])
                    return ins
                T.op("pe", fn, [("Otok", lt), "ident"], [("ps", b)])
                copy_op(eveng(), mT[:, 2 * g:2 * g + 2, lt * 128:lt * 128 + rows],
                        psb[:, 0:256].rearrange("p (k n) -> p k n", n=128)[:, :, 0:rows], [("ps", b)], [("mT", lt)])

        def attention(h):
            Kv = Kg(); Vv = Vg(); Qv = Qg()
            npast = 8 * h
            st["npb"] = 6
            cast_eng = "dve" if h == 1 else "act"
            T.op("dve", lambda e: e.memset(Vv[:, :, :, 64:65], 1.0), [],
                 ["gTfree"] + [("Vg", i, hh) for i in range(17) for hh in range(4)] + [("Kg", i) for i in range(17)] + [("Qg", i) for i in range(9)]
                 + [("gT", f, c) for f in range(12) for c in (0, 512, 1024)])
            for g in range(4):
                if h == 1:
                    T.op("dve", lambda e: e.memset(Vv[:, 0:16, :, 64:65], 1.0), ["gTfree"], [("Vg", i, hh) for i in range(16) for hh in range(4)])
                    T.dma("pool", Kv[:, 0:8, :], kp_d[0:1024, g * 256:(g + 1) * 256].rearrange("(t p) c -> p t c", p=128),
                          ["gTfree"] + [("kp_out", g, i) for i in range(8)], [("Kg", i) for i in range(8)])
                    for hh in range(4):
                        T.dma("pool", Vv[:, 0:8, hh, 0:64], vp_d[0:1024, g * 256 + hh * 64:g * 256 + (hh + 1) * 64].rearrange("(t p) c -> p t c", p=128),
                              ["gTfree"] + [("vp_out", g, i) for i in range(8)], [("Vg", i, hh) for i in range(8)])
                if h == 1:
                    T.dma("pool", Kc[:, :, :], ck_d[:, g * 256:(g + 1) * 256].rearrange("(t p) c -> p t c", p=128), [], ["Kc"])
                sa = ring.acquire(0)
                wa = ringb[sa][:, :].rearrange("p (kc n) -> p kc n", kc=8)
                sb_ = ring.acquire(1)
                wb = ringb[sb_][:, 0:8 * 272].rearrange("p (kc n) -> p kc n", kc=8)
                nB = 272 if g == 0 else 256
                tails = []
                st2 = []
                for (lt, rows) in tiles(h):
                    need_xnT(lt)
                    is_s = (lt == 8)
                    kidx = 16 if is_s else npast + lt
                    bA = pbank(); bB = pbank()
                    mm_group(ps[bA][:rows, 0:512], [(xnT[:, kc, lt * 128:lt * 128 + rows], wa[:, kc, :]) for kc in range(8)],
                             RK(sa) + [("xnT", lt)], [("ps", bA)])
                    mm_group(ps[bB][:rows, 0:nB], [(xnT[:, kc, lt * 128:lt * 128 + rows], wb[:, kc, 0:nB]) for kc in range(8)],
                             RK(sb_) + [("xnT", lt)], [("ps", bB)])
                    T.op("act", lambda e: e.activation(out=sq[:rows, :], in_=ps[bA][:rows, :], func=AF.Square), [("ps", bA)], ["sq"])
                    si = rr("ss", 6)
                    sm = small[si]
                    T.op("dve", lambda e: e.reduce_sum(out=sm[:rows, 0:8], in_=sq[:rows, :].rearrange("p (h c) -> p h c", c=64), axis=AX.X),
                         ["sq"], [("sm", si)])
                    T.op("act", lambda e: e.activation(out=sm[:rows, 8:16], in_=sm[:rows, 0:8], func=AF.Ln, scale=1.0 / HD, bias=epsT[:rows, 0:1]),
                         [("sm", si), "eps"], [("sm", si)])
                    T.op("act", lambda e: e.activation(out=sm[:rows, 16:24], in_=sm[:rows, 8:16], func=AF.Exp, scale=-0.5), [("sm", si)], [("sm", si)])
                    def stage2(lt=lt, rows=rows, is_s=is_s, kidx=kidx, bA=bA, bB=bB, si=si, sm=sm):
                        t1 = rr("stmp", 3)
                        T.op("dve", lambda e: e.tensor_tensor(out=stmp[t1][:rows, 0:256].rearrange("p (h c) -> p h c", c=64),
                                                              in0=ps[bA][:rows, 0:256].rearrange("p (h c) -> p h c", c=64),
                                                              in1=sm[:rows, 16:20].unsqueeze(2).to_broadcast([rows, 4, 64]), op=ALU.mult),
                             [("ps", bA), ("sm", si)], [("stmp", t1)])
                        T.op("dve", lambda e: e.tensor_tensor(out=Qv[:rows, lt, :, 0:64], in0=stmp[t1][:rows, 0:256].rearrange("p (h c) -> p h c", c=64),
                                                              in1=gq8[:rows, :].unsqueeze(1).to_broadcast([rows, 4, 64]), op=ALU.mult),
                             [("stmp", t1), "gq8"], [("Qg", lt)])
                        t2 = rr("stmp", 3)
                        T.op("dve", lambda e: e.tensor_tensor(out=stmp[t2][:rows, 0:256].rearrange("p (h c) -> p h c", c=64),
                                                              in0=ps[bA][:rows, 256:512].rearrange("p (h c) -> p h c", c=64),
                                                              in1=sm[:rows, 20:24].unsqueeze(2).to_broadcast([rows, 4, 64]), op=ALU.mult),
                             [("ps", bA), ("sm", si)], [("stmp", t2)])
                        ki = rr("kst", 3)
                        kb, kk = (kst[ki], ("kst", ki)) if ki < 2 else (ub[0][:, 0:256], ("ub", 0))
                        vb, vk = (vst[ki], ("vst", ki)) if ki < 2 else (ub[1][:, 0:256], ("ub", 1))
                        T.op("dve", lambda e: e.tensor_tensor(out=kb[:rows, :].rearrange("p (h c) -> p h c", c=64),
                                                              in0=stmp[t2][:rows, 0:256].rearrange("p (h c) -> p h c", c=64),
                                                              in1=gkT[:rows, :].unsqueeze(1).to_broadcast([rows, 4, 64]), op=ALU.mult),
                             [("stmp", t2), "gkT"], [kk])
                        copy_op(cast_eng, Kv[:rows, kidx, :], kb[:rows, :], [kk], [("Kg", kidx)])
                        copy_op("act", vb[:rows, :], ps[bB][:rows, 0:256], [("ps", bB)], [vk])
                        copy_op(cast_eng, Vv[:rows, kidx, :, 0:64], ps[bB][:rows, 0:256].rearrange("p (h c) -> p h c", c=64),
                                [("ps", bB)], [("Vg", kidx, hh) for hh in range(4)])
                        if is_s:
                            T.dma("sp", ks_d[:, g * 256:(g + 1) * 256], kb[:rows, :], [kk], [], is_out=True)
                            T.dma("sp", vs_d[:, g * 256:(g + 1) * 256], vb[:rows, :], [vk], [], is_out=True)
                        else:
                            r0 = (h * 8 + lt) * 128
                            T.dma("sp", kp_d[r0:r0 + 128, g * 256:(g + 1) * 256], kb[:, :], [kk], [("kp_out", g, lt)] if h == 0 else [], is_out=True)
                            T.dma("sp", vp_d[r0:r0 + 128, g * 256:(g + 1) * 256], vb[:, :], [vk], [("vp_out", g, lt)] if h == 0 else [], is_out=True)
                        if g == 0:
                            li = rr("lf", 5)
                            lf = lfst[li]
                            s2 = rr("ss", 6)
                            T.op("dve", lambda e: e.tensor_tensor(out=small[s2][:rows, 0:16], in0=ps[bB][:rows, 256:272], in1=bfT[:rows, :], op=ALU.add),
                                 [("ps", bB), "bfT"], [("sm", s2)])
                            T.op("act", lambda e: e.activation(out=small[s2][:rows, 16:32], in_=small[s2][:rows, 0:16], func=AF.Exp, scale=-1.0),
                                 [("sm", s2)], [("sm", s2)])
                            T.op("act", lambda e: e.activation(out=small[s2][:rows, 32:48], in_=small[s2][:rows, 16:32], func=AF.Ln, bias=oneT[:rows, 0:1]),
                                 [("sm", s2), "oneT"], [("sm", s2)])
                            T.op("dve", lambda e: e.tensor_scalar(out=lf[:rows, :], in0=small[s2][:rows, 32:48], scalar1=-1.0, scalar2=None, op0=ALU.mult),
                                 [("sm", s2)], [("lf", li)])
                            if is_s:
                                T.dma("sp", lfs_d[:, :], lf[:rows, :], [("lf", li)], [], is_out=True)
                            else:
                                T.dma("sp", lfp_d[(h * 8 + lt) * 128:(h * 8 + lt + 1) * 128, :], lf[:, :], [("lf", li)], [], is_out=True)

                        def tail(lt=lt, rows=rows, is_s=is_s, li=(li if g == 0 else None)):
                            if g == 0:
                                if is_s:
                                    cumsum_tile(lfst[li][:rows, :], rows, R_s, "R_s", c_s[:rows, 16, :], ("c_s", 16), [("lf", li)])
                                else:
                                    gt_ = h * 8 + lt
                                    cumsum_tile(lfst[li][:rows, :], rows, R_p, "R_p", c_p[:rows, gt_, :], ("c_p", gt_), [("lf", li)])
                        tails.append(tail)
                        while len(tails) > 3:
                            tails.pop(0)()
                    st2.append(stage2)
                    while len(st2) > 1:
                        st2.pop(0)()
                while st2:
                    st2.pop(0)()
                while tails:
                    tails.pop(0)()
                pk = [("c_p", h * 8 + i) for i in range(8)]
                if g == 0:
                    cumsum_post(128, 8, c_p[:, h * 8:h * 8 + 8, :], nb_p[:, h * 8:h * 8 + 8, :], pk, cs_v=cs_p[:, h * 8:h * 8 + 8, :, :])
                    if h == 1:
                        cumsum_post(64, 1, c_s[:64, 16:17, :], nb_s[:64, 16:17, :], [("c_s", 16)], cs_v=cs_s[:64, 0:1, :, :])
                T.op("dve", lambda e: e.tensor_copy(out=Qv[:, 0:8, :, 64:67], in_=cs_p[:, h * 8:h * 8 + 8, g * 4:(g + 1) * 4, :]),
                     [(k, c) for k in pk for c in ("cs", "cs1", "cs2")], [("Qg", lt) for lt in range(8)])
                if h == 1:
                    T.op("dve", lambda e: e.tensor_copy(out=Qv[:64, 8, :, 64:67], in_=cs_s[:64, 0, g * 4:(g + 1) * 4, :]),
                         [(("c_s", 16), c) for c in ("cs", "cs1", "cs2")], [("Qg", 8)])
                ring.release()
                ring.release()
                ktl = [(i, 128, i, None) for i in range(npast)] + [(npast + i, 128, npast + i, i) for i in range(8)]
                attn_job(g, [(lt, 128) for lt in range(8)], ktl, nb_p, "c_p")
                if h == 1:
                    for hh in range(4):
                        T.dma("pool", Vv[:, 0:16, hh, 0:64], cv_d[:, g * 256 + hh * 64:g * 256 + (hh + 1) * 64].rearrange("(t p) c -> p t c", p=128),
                              [], [("Vg", i, hh) for i in range(16)])
                    for hh in range(4):
                        T.op("dve", lambda e: e.tensor_tensor(out=Vv[:, 0:16, hh, 0:65], in0=Vv[:, 0:16, hh, 0:65],
                                                              in1=nb_s[:, 0:16, g * 4 + hh:g * 4 + hh + 1].to_broadcast([128, 16, 65]), op=ALU.mult),
                             ["w_s"] + [("Vg", i, hh) for i in range(16)], [("Vg", i, hh) for i in range(16)])
                    ktl = [(i, 128, i, None) for i in range(16)] + [(16, 64, 16, 0)]
                    attn_job(g, [(8, 64)], ktl, nb_s, "c_s", batch=True)

        for lt in range(4):
            T.dma("sp", x[:, lt, :], xp_d[lt * 128:(lt + 1) * 128, :], [], [("x", lt)])
        NV = [nffn_d[0, 0], nmix_d[0], nffn_d[0, 1], nffn_d[1, 0], nmix_d[1], nffn_d[1, 1]] * 2
        gks = {}

        def load_norm(i):
            if i < len(NV):
                gks[i] = start_norm(NV[i])
        load_norm(0)
        gk0 = gks[0]
        T._waits("sp", [("x", lt) for lt in range(4)], [])
        for lt in range(4, 8):
            T.dma("sp", x[:, lt, :], xp_d[lt * 128:(lt + 1) * 128, :], [], [("x", lt)])
        T.dma("sp", x[:64, 8, :], xs_d[:, :], [], [("x", 8)])
        load_norm(1)
        for n_ in (1, 2, 3):
            ring.n = n_
            ring.prefetch()
            if n_ < 3:
                T._waits("pool", RK(n_ - 1), [])
        wor.prefetch()
        consts_late()
        for t in range(16):
            prologue_q.append(lambda t=t: cumsum_tile(lfc[:, t, :], 128, R_s, "R_s", c_s[:, t, :], ("c_s", t), ["lfc"]))

        def prologue_tail():
            bce = pbank()
            T.op("pe", lambda pe: pe.matmul(ps[bce][:, 0:16], ones[:, :], R_s[:, :], start=True, stop=True), ["R_s", "ones"], [("ps", bce)])
            T.op("dve", lambda e: e.tensor_tensor(out=nb_s[:, 0:16, :], in0=c_s[:, 0:16, :], in1=ps[bce][:, 0:16].unsqueeze(1).to_broadcast([128, 16, 16]),
                                                  op=ALU.subtract), [("ps", bce)] + [("c_s", t) for t in range(16)], ["w_s"])
            T.op("act", lambda e: e.activation(out=nb_s[:, 0:16, :], in_=nb_s[:, 0:16, :], func=AF.Exp, scale=-1.0), ["w_s"], ["w_s"])
            T.op("dve", lambda e: e.memset(R_s[:], 0.0), [], ["R_s"])
        prologue_q.append(prologue_tail)

        for h in range(2):
            if h == 0:
                gk = gk0
                for (lt, rows) in tiles(h):
                    norm_tile(lt, rows, gk)
                    flush_pending(keep=2)
            else:
                norm_tile(8, 64, gk_next)
            base = 6 * h
            load_norm(base + 2)
            ffn(h, ("norm", gks[base + 1]))
            load_norm(base + 3)
            conv(h)
            out_proj(h, ("norm", gks[base + 2]))
            load_norm(base + 4)
            if h == 1:
                for t in range(2):
                    T.dma("sp", csp_d[t].rearrange("(dc p) -> p dc", p=128), histp[:, t, :], ["histp"], [], is_out=True,
                          allow_slow_non_contiguous=True)
                    T.dma("sp", css_d[t].rearrange("(dc p) -> p dc", p=128), hsout[:, t, :], ["hsout"], [], is_out=True,
                          allow_slow_non_contiguous=True)
            ffn(h, ("norm", gks[base + 3]))
            load_norm(base + 5)
            ffn(h, ("norm", gks[base + 4]))
            flush_pending()
            while prologue_q:
                prologue_q.pop(0)()
            load_norm(base + 6)
            attention(h)
            st["npb"] = 8
            out_proj(h, ("norm", gks[base + 5]))
            load_norm(base + 7)
            gk_next = gks.get(base + 6)
            ffn(h, ("out", h, gk_next))
            while reload_q:
                do_reload(reload_q.pop(0))
            while norm_q:
                a = norm_q.pop(0)
                flush_pending(keep=2)
                norm_tile(*a)
        T.finish()
    return nc


_NC = None


def kernel(x_prompt, x_sample, state_conv, cache_k, cache_v, cache_logf, norm_ffn, ffn_w_in, ffn_w_out, norm_mix,
           conv_w_in, conv_w, conv_w_out, attn_w_in, attn_b_f, q_norm, k_norm, attn_w_out):
    global _NC
    if _NC is None:
        _NC = build_nc()
    nc = _NC
    f = lambda a: np.ascontiguousarray(np.asarray(a, dtype=np.float32))
    shared = {
        "norm_ffn": f(norm_ffn), "ffn_w_in": f(ffn_w_in), "ffn_w_out": f(ffn_w_out), "norm_mix": f(norm_mix),
        "conv_w_in": f(conv_w_in)[0], "conv_w": f(conv_w)[0], "conv_w_out": f(conv_w_out)[0],
        "attn_w_in": f(attn_w_in)[0], "attn_b_f": f(attn_b_f)[0], "q_norm": f(q_norm)[0], "k_norm": f(k_norm)[0],
        "attn_w_out": f(attn_w_out)[0],
    }
    xp = f(x_prompt); xs = f(x_sample); sc = f(state_conv); ck = f(cache_k); cv = f(cache_v); cl = f(cache_logf)
    in_maps = []
    for b in range(NCORES):
        m = dict(shared)
        m.update({"xp": xp[b], "xs": xs[b], "sconv": sc[0, b], "ck": ck[0, b].reshape(PL, D), "cv": cv[0, b].reshape(PL, D),
                  "clf": cl[0, b]})
        in_maps.append(m)
    res = run_bass_kernel_spmd(nc, in_maps, core_ids=list(range(NCORES)))
    R = res.results
    st = lambda k: np.stack([np.asarray(R[b][k], dtype=np.float32) for b in range(NCORES)])
    yp = st("yp"); ys = st("ys")
    csp = st("csp")[None]; css = st("css")[None]
    kp = st("kp").reshape(1, NCORES, PL, H, HD); vp = st("vp").reshape(1, NCORES, PL, H, HD); lfp = st("lfp")[None]
    ks = st("ks").reshape(1, NCORES, SL, H, HD); vs = st("vs").reshape(1, NCORES, SL, H, HD); lfs = st("lfs")[None]
    return (yp, ys, csp, css, kp, vp, lfp, ks, vs, lfs)
```
